# Optimizing a Trainium2 kernel written in Bass

```python
import math
import jax, jax.numpy as jnp
from jax import lax
import numpy as np

D_MODEL = 2048
BATCH = 4
SEQ = 2048
DEPTH = 4
DEC_BATCH = 32
DEC_SEQ = 1
PAST_LEN = 16384
PAGE_SIZE = 128

MIX_WIDTH = D_MODEL
ATTN_WIDTH = MIX_WIDTH // 2
SSM_WIDTH = MIX_WIDTH - ATTN_WIDTH
HEAD_DIM = 64
N_HEADS = ATTN_WIDTH // HEAD_DIM
N_KV_HEADS = max(1, N_HEADS // 8)
KV_REP = N_HEADS // N_KV_HEADS
WINDOW = 128
ROPE_THETA = 10000.0
SSM_GROUP = 16
SSM_GROUPS = SSM_WIDTH // SSM_GROUP
SSM_STATE = 64
D_FF = ((8 * D_MODEL // 3 + 255) // 256) * 256
CONV_W = 3
RMS_EPS = 1e-6
KV_COLS = N_KV_HEADS * HEAD_DIM
IN_COLS = ATTN_WIDTH + 2 * KV_COLS + SSM_WIDTH
NEG_INF = -1e30

kernel_name = "hymba_swa_s5_convffn_decode_step"


def rmsnorm(x, g):
    xf = x.astype(jnp.float32)
    y = xf * lax.rsqrt(jnp.mean(xf * xf, axis=-1, keepdims=True) + RMS_EPS)
    return (y * g.astype(jnp.float32)).astype(x.dtype)


def rope(x, pos):
    half = HEAD_DIM // 2
    inv = ROPE_THETA ** (-jnp.arange(half, dtype=jnp.float32) / half)
    ang = pos.astype(jnp.float32)[:, None] * inv[None, :]
    cos = jnp.cos(ang)[None, :, None, :]
    sin = jnp.sin(ang)[None, :, None, :]
    xf = x.astype(jnp.float32)
    x1, x2 = xf[..., :half], xf[..., half:]
    out = jnp.concatenate([x1 * cos - x2 * sin, x2 * cos + x1 * sin], axis=-1)
    return out.astype(x.dtype)


def sink_attention(q, k, v, mask, sinks):
    s = jnp.einsum("...qhrd,...khd->...hrqk", q.astype(jnp.float32), k.astype(jnp.float32)) * (HEAD_DIM ** -0.5)
    s = jnp.where(mask, s, NEG_INF)
    sink = sinks.astype(jnp.float32).reshape(N_KV_HEADS, KV_REP, 1, 1)
    m = jnp.maximum(jnp.max(s, axis=-1, keepdims=True), sink)
    p = jnp.exp(s - m)
    denom = jnp.sum(p, axis=-1, keepdims=True) + jnp.exp(sink - m)
    o = jnp.einsum("...hrqk,...khd->...qhrd", p / denom, v.astype(jnp.float32))
    return o.astype(q.dtype)


def attn_prompt(q, k, v, sinks):
    B, L = q.shape[:2]
    nb = L // WINDOW
    qb = q.reshape(B, nb, WINDOW, N_KV_HEADS, KV_REP, HEAD_DIM)

    def band(t):
        tb = t.reshape(B, nb, WINDOW, N_KV_HEADS, HEAD_DIM)
        prev = jnp.concatenate([jnp.zeros_like(tb[:, :1]), tb[:, :-1]], axis=1)
        return jnp.concatenate([prev, tb], axis=2)

    qi = jnp.arange(WINDOW)[:, None]
    kj = jnp.arange(2 * WINDOW)[None, :]
    diff = qi + WINDOW - kj
    blk = jnp.arange(nb)[:, None, None]
    valid = (diff >= 0) & (diff <= WINDOW) & (blk * WINDOW - WINDOW + kj >= 0)
    mask = valid[None, :, None, None]
    o = sink_attention(qb, band(k), band(v), mask, sinks)
    return o.reshape(B, L, ATTN_WIDTH)


def attn_sample(q, k, v, buf_k, buf_v, sinks):
    DB, S = q.shape[:2]
    WB = buf_k.shape[1]
    keys = jnp.concatenate([buf_k.astype(k.dtype), k], axis=1)
    vals = jnp.concatenate([buf_v.astype(v.dtype), v], axis=1)
    t = PAST_LEN + jnp.arange(S)
    spos = jnp.concatenate([PAST_LEN - WB + jnp.arange(WB), PAST_LEN + jnp.arange(S)])
    diff = t[:, None] - spos[None, :]
    mask = ((diff >= 0) & (diff <= WINDOW))[None, None, None]
    o = sink_attention(q.reshape(DB, S, N_KV_HEADS, KV_REP, HEAD_DIM), keys, vals, mask, sinks)
    return o.reshape(DB, S, ATTN_WIDTH), keys[:, -WB:], vals[:, -WB:]


def cmul(ar, ai, br, bi):
    return ar * br - ai * bi, ar * bi + ai * br


def s5_mixer(u, a_re, a_im, b_re, b_im, c_re, c_im, d, log_dt, w_glu, b_glu, h0=None):
    f32 = jnp.float32
    Bt, L = u.shape[:2]
    uf = u.astype(f32).reshape(Bt, L, SSM_GROUPS, SSM_GROUP)
    dt = jnp.exp(log_dt.astype(f32))[:, None]
    ar, ai = a_re.astype(f32), a_im.astype(f32)
    mag = jnp.exp(ar * dt)
    lr, li = mag * jnp.cos(ai * dt), mag * jnp.sin(ai * dt)
    nr, ni = lr - 1.0, li
    den = ar * ar + ai * ai
    qr = (nr * ar + ni * ai) / den
    qi = (ni * ar - nr * ai) / den
    bbr, bbi = cmul(qr[..., None], qi[..., None], b_re.astype(f32), b_im.astype(f32))
    xr = jnp.einsum("blgc,gpc->blgp", uf, bbr)
    xi = jnp.einsum("blgc,gpc->blgp", uf, bbi)
    if h0 is not None:
        pr, pi_ = cmul(lr, li, h0[0].astype(f32), h0[1].astype(f32))
        xr = xr.at[:, 0].add(pr)
        xi = xi.at[:, 0].add(pi_)
    a_r = jnp.broadcast_to(lr, (1, L) + lr.shape)
    a_i = jnp.broadcast_to(li, (1, L) + li.shape)

    def combine(e1, e2):
        a1r, a1i, b1r, b1i = e1
        a2r, a2i, b2r, b2i = e2
        nar, nai = cmul(a2r, a2i, a1r, a1i)
        nbr, nbi = cmul(a2r, a2i, b1r, b1i)
        return nar, nai, nbr + b2r, nbi + b2i

    _, _, hr, hi = lax.associative_scan(combine, (a_r, a_i, xr, xi), axis=1)
    y = (jnp.einsum("blgp,gcp->blgc", hr, c_re.astype(f32))
         - jnp.einsum("blgp,gcp->blgc", hi, c_im.astype(f32))
         + d.astype(f32).reshape(SSM_GROUPS, SSM_GROUP) * uf)
    z = jax.nn.gelu(y.reshape(Bt, L, SSM_WIDTH)).astype(u.dtype)
    out = z * jax.nn.sigmoid(z @ w_glu + b_glu)
    return out, hr[:, -1], hi[:, -1]


def conv_ffn(h, w_up, conv_w, conv_b, w_down, buf=None):
    up = h @ w_up
    Bt, L = up.shape[:2]
    if buf is None:
        pad = jnp.zeros((Bt, CONV_W - 1, up.shape[-1]), up.dtype)
    else:
        pad = buf.astype(up.dtype)
    ext = jnp.concatenate([pad, up], axis=1)
    conv = conv_b
    for j in range(CONV_W):
        conv = conv + conv_w[j] * ext[:, j:j + L]
    g, val = jnp.split(conv, 2, axis=-1)
    out = (jax.nn.silu(g) * val) @ w_down
    return out, ext[:, -(CONV_W - 1):]


def trunk_layer(x, pos, p, l, cache=None):
    Bt, L = x.shape[:2]
    h = rmsnorm(x, p["attn_norm_g"][l])
    proj = h @ p["w_in"][l]
    q, k, v, u = jnp.split(proj, [ATTN_WIDTH, ATTN_WIDTH + KV_COLS, ATTN_WIDTH + 2 * KV_COLS], axis=-1)
    q = rope(q.reshape(Bt, L, N_HEADS, HEAD_DIM), pos)
    k = rope(k.reshape(Bt, L, N_KV_HEADS, HEAD_DIM), pos)
    v = v.reshape(Bt, L, N_KV_HEADS, HEAD_DIM)
    sinks = p["attn_sinks"][l]
    if cache is None:
        a = attn_prompt(q, k, v, sinks)
        wb = min(WINDOW, L)
        nk, nv = k[:, -wb:], v[:, -wb:]
        h0, cbuf = None, None
    else:
        buf_k, buf_v, h0r, h0i, cbuf = cache
        a, nk, nv = attn_sample(q, k, v, buf_k, buf_v, sinks)
        h0 = (h0r, h0i)
    s, hr, hi = s5_mixer(u, p["ssm_a_re"][l], p["ssm_a_im"][l], p["ssm_b_re"][l], p["ssm_b_im"][l],
                         p["ssm_c_re"][l], p["ssm_c_im"][l], p["ssm_d"][l], p["ssm_log_dt"][l],
                         p["w_glu"][l], p["b_glu"][l], h0)
    mix = jnp.concatenate([rmsnorm(a, p["attn_out_norm_g"][l]), rmsnorm(s, p["ssm_out_norm_g"][l])], axis=-1)
    x = x + mix @ p["w_out"][l]
    f, nconv = conv_ffn(rmsnorm(x, p["ffn_norm_g"][l]), p["w_up"][l], p["conv_w"][l], p["conv_b"][l],
                        p["w_down"][l], cbuf)
    x = x + f
    return x, (nk, nv, hr, hi, nconv)


def setup_inputs(seed: int = 0) -> dict:
    key = jax.random.key(seed)
    ks = jax.random.split(key, 32)
    f32 = jnp.float32

    def nrm(k, shape, scale):
        return jax.random.normal(k, shape, f32) * scale

    w_buf = min(WINDOW, PAST_LEN)
    n = jnp.arange(SSM_STATE, dtype=f32)
    G, P = SSM_GROUPS, SSM_STATE
    return {
        "x_prompt": nrm(ks[0], (BATCH, SEQ, D_MODEL), 1.0),
        "x_sample": nrm(ks[1], (DEC_BATCH, DEC_SEQ, D_MODEL), 1.0),
        "cache_k": nrm(ks[2], (DEPTH, DEC_BATCH, w_buf, N_KV_HEADS, HEAD_DIM), 1.0),
        "cache_v": nrm(ks[3], (DEPTH, DEC_BATCH, w_buf, N_KV_HEADS, HEAD_DIM), 1.0),
        "state_ssm_re": nrm(ks[4], (DEPTH, DEC_BATCH, G, P), 0.5),
        "state_ssm_im": nrm(ks[5], (DEPTH, DEC_BATCH, G, P), 0.5),
        "state_conv": nrm(ks[6], (DEPTH, DEC_BATCH, CONV_W - 1, 2 * D_FF), 1.0),
        "attn_norm_g": 1.0 + nrm(ks[7], (DEPTH, D_MODEL), 0.02),
        "w_in": nrm(ks[8], (DEPTH, D_MODEL, IN_COLS), D_MODEL ** -0.5),
        "attn_sinks": nrm(ks[9], (DEPTH, N_HEADS), 0.5),
        "ssm_a_re": -0.5 + nrm(ks[10], (DEPTH, G, P), 0.01),
        "ssm_a_im": math.pi * n + nrm(ks[11], (DEPTH, G, P), 0.01),
        "ssm_b_re": nrm(ks[12], (DEPTH, G, P, SSM_GROUP), (2 * SSM_GROUP) ** -0.5),
        "ssm_b_im": nrm(ks[13], (DEPTH, G, P, SSM_GROUP), (2 * SSM_GROUP) ** -0.5),
        "ssm_c_re": nrm(ks[14], (DEPTH, G, SSM_GROUP, P), (2 * P) ** -0.5),
        "ssm_c_im": nrm(ks[15], (DEPTH, G, SSM_GROUP, P), (2 * P) ** -0.5),
        "ssm_d": nrm(ks[16], (DEPTH, SSM_WIDTH), 0.5),
        "ssm_log_dt": jax.random.uniform(ks[17], (DEPTH, G), f32, math.log(1e-3), math.log(1e-1)),
        "w_glu": nrm(ks[18], (DEPTH, SSM_WIDTH, SSM_WIDTH), SSM_WIDTH ** -0.5),
        "b_glu": nrm(ks[19], (DEPTH, SSM_WIDTH), 0.01),
        "attn_out_norm_g": 1.0 + nrm(ks[20], (DEPTH, ATTN_WIDTH), 0.02),
        "ssm_out_norm_g": 1.0 + nrm(ks[21], (DEPTH, SSM_WIDTH), 0.02),
        "w_out": nrm(ks[22], (DEPTH, MIX_WIDTH, D_MODEL), MIX_WIDTH ** -0.5),
        "ffn_norm_g": 1.0 + nrm(ks[23], (DEPTH, D_MODEL), 0.02),
        "w_up": nrm(ks[24], (DEPTH, D_MODEL, 2 * D_FF), D_MODEL ** -0.5),
        "conv_w": nrm(ks[25], (DEPTH, CONV_W, 2 * D_FF), CONV_W ** -0.5),
        "conv_b": nrm(ks[26], (DEPTH, 2 * D_FF), 0.01),
        "w_down": nrm(ks[27], (DEPTH, D_FF, D_MODEL), D_FF ** -0.5),
        "final_norm_g": 1.0 + nrm(ks[28], (D_MODEL,), 0.02),
    }


def reference(x_prompt, x_sample, cache_k, cache_v, state_ssm_re, state_ssm_im, state_conv,
              attn_norm_g, w_in, attn_sinks, ssm_a_re, ssm_a_im, ssm_b_re, ssm_b_im, ssm_c_re, ssm_c_im,
              ssm_d, ssm_log_dt, w_glu, b_glu, attn_out_norm_g, ssm_out_norm_g, w_out, ffn_norm_g,
              w_up, conv_w, conv_b, w_down, final_norm_g):
    p = dict(attn_norm_g=attn_norm_g, w_in=w_in, attn_sinks=attn_sinks, ssm_a_re=ssm_a_re, ssm_a_im=ssm_a_im,
             ssm_b_re=ssm_b_re, ssm_b_im=ssm_b_im, ssm_c_re=ssm_c_re, ssm_c_im=ssm_c_im, ssm_d=ssm_d,
             ssm_log_dt=ssm_log_dt, w_glu=w_glu, b_glu=b_glu, attn_out_norm_g=attn_out_norm_g,
             ssm_out_norm_g=ssm_out_norm_g, w_out=w_out, ffn_norm_g=ffn_norm_g, w_up=w_up,
             conv_w=conv_w, conv_b=conv_b, w_down=w_down)
    pos_p = jnp.arange(x_prompt.shape[1], dtype=jnp.int32)
    pos_s = PAST_LEN + jnp.arange(x_sample.shape[1], dtype=jnp.int32)
    xp, xs = x_prompt, x_sample
    st_p = []
    st_s = []
    for l in range(DEPTH):
        xp, sp = trunk_layer(xp, pos_p, p, l)
        xs, ss = trunk_layer(xs, pos_s, p, l,
                             (cache_k[l], cache_v[l], state_ssm_re[l], state_ssm_im[l], state_conv[l]))
        st_p.append(sp)
        st_s.append(ss)
    y_prompt = rmsnorm(xp, final_norm_g)
    y_sample = rmsnorm(xs, final_norm_g)
    k_prompt = jnp.stack([s[0] for s in st_p], 0)
    v_prompt = jnp.stack([s[1] for s in st_p], 0)
    ssm_re_prompt = jnp.stack([s[2] for s in st_p], 0)
    ssm_im_prompt = jnp.stack([s[3] for s in st_p], 0)
    conv_prompt = jnp.stack([s[4] for s in st_p], 0)
    k_sample = jnp.stack([s[0] for s in st_s], 0)
    v_sample = jnp.stack([s[1] for s in st_s], 0)
    ssm_re_sample = jnp.stack([s[2] for s in st_s], 0)
    ssm_im_sample = jnp.stack([s[3] for s in st_s], 0)
    conv_sample = jnp.stack([s[4] for s in st_s], 0)
    return (y_prompt, y_sample, k_prompt, v_prompt, ssm_re_prompt, ssm_im_prompt, conv_prompt,
            k_sample, v_sample, ssm_re_sample, ssm_im_sample, conv_sample)
```

```python
import math
from contextlib import ExitStack
import numpy as np
import concourse.bass as bass
import concourse.mybir as mybir
from concourse.bass_utils import run_bass_kernel_spmd

F32 = mybir.dt.float32
BF16 = mybir.dt.bfloat16
I32 = mybir.dt.int32
ALU = mybir.AluOpType
AF = mybir.ActivationFunctionType
ESZ = {F32: 4, BF16: 2, I32: 4}

L = 4
D = 2048
NCH = 16
NT = 512
NTILES = 4
SEQ = 2048
NS = 4
DFF = 5632
INC = 2304
G = 64
P = 64
NCORES = 8


MAXMT = None
ASTOP = None
_acount = [0]


def astep():
    _acount[0] += 1
    if ASTOP is not None and _acount[0] >= ASTOP:
        raise _Stop()


NOSAMP = False
LIMIT = None
DBG = False


class _Stop(Exception):
    pass


class Sched:
    K = 8
    ENGMAP = {"pe": "tensor", "dve": "vector", "act": "scalar", "pool": "gpsimd", "sp": "sync"}

    def __init__(self, nc):
        self.nc = nc
        self.streams = {}
        self.lastw = {}
        self.readers = {}
        self.track = set()

    def _keys(self, ap):
        space = str(ap.space)
        apl = list(ap.ap)
        esz = ESZ[ap.dtype]
        name = ap.tensor.name
        if "SB" not in space and "PSUM" not in space:
            if name not in self.track:
                return []
            lo = ap.offset
            hi = lo
            for s, c in apl:
                hi += (c - 1) * abs(s)
            return [(name, g) for g in range(lo * esz // 65536, hi * esz // 65536 + 1)]
        ps = apl[0][0]
        lo = ap.offset % ps if ps > 0 else ap.offset
        hi = lo
        for s, c in apl[1:]:
            hi += (c - 1) * abs(s)
        gran = 2048 if "PSUM" in space else 256
        return [(name, g) for g in range(lo * esz // gran, hi * esz // gran + 1)]

    def add(self, stream, fn, reads, writes, dma=False):
        st = self.streams.setdefault(stream, [])
        seq = len(st)
        deps = {}

        def dep(w):
            if w is None:
                return
            s2, q2 = w
            if s2 == stream and stream == "pe":
                return
            if self.streams[s2][q2]["dma"]:
                deps[(s2, q2)] = q2
            elif deps.get(s2, -1) < q2:
                deps[s2] = q2

        rk = [k for ap in reads for k in self._keys(ap)]
        wk = [k for ap in writes for k in self._keys(ap)]
        for k in rk:
            dep(self.lastw.get(k))
        for k in wk:
            dep(self.lastw.get(k))
            for s2, q2 in self.readers.get(k, {}).items():
                dep((s2, q2))
        for k in rk:
            self.readers.setdefault(k, {})[stream] = seq
        for k in wk:
            self.lastw[k] = (stream, seq)
            self.readers[k] = {}
        st.append(dict(fn=fn, deps=deps, dma=dma, needed=False))

    def emit(self, stack):
        nc = self.nc
        K = self.K
        for st in self.streams.values():
            for op in st:
                for s2, q2 in op["deps"].items():
                    s2 = s2[0] if isinstance(s2, tuple) else s2
                    self.streams[s2][q2]["needed"] = True
        csem = {}
        dsem = {}
        for name, st in self.streams.items():
            c = 0
            d = 0
            for op in st:
                if op["dma"]:
                    op["didx"] = d
                    d += 1
                else:
                    if op["needed"]:
                        c += 1
                    op["cnt"] = c
            if c > 0:
                csem[name] = stack.enter_context(nc.semaphore("c_" + name))
            if d > 0:
                dsem[name] = [stack.enter_context(nc.semaphore("d%d_%s" % (i, name))) for i in range(min(K, d))]
        streams = self.streams

        def run(name, eng):
            seen = {}

            def wait(sem, key, val):
                if seen.get(key, 0) >= val:
                    return
                eng.wait_ge(sem, val)
                seen[key] = val

            for op in streams[name]:
                for s2, q2 in op["deps"].items():
                    s2 = s2[0] if isinstance(s2, tuple) else s2
                    o2 = streams[s2][q2]
                    if o2["dma"]:
                        n = o2["didx"]
                        wait(dsem[s2][n % K], (s2, n % K), 16 * (n // K + 1))
                    else:
                        wait(csem[s2], (s2, "c"), o2["cnt"])
                if op["dma"]:
                    n = op["didx"]
                    if n >= K:
                        wait(dsem[name][n % K], (name, n % K), 16 * (n // K))
                    ins = op["fn"](eng)
                    ins.then_inc(dsem[name][n % K], 16)
                else:
                    ins = op["fn"](eng)
                    if op["needed"]:
                        ins.then_inc(csem[name], 1)
            nd = sum(1 for op in streams[name] if op["dma"])
            if nd > 0:
                for i in range(min(K, nd)):
                    cnt = (nd - i + K - 1) // K
                    wait(dsem[name][i], (name, i), 16 * cnt)

        block = stack.enter_context(nc.Block())
        for name in streams:
            getattr(block, self.ENGMAP[name])(lambda eng, name=name: run(name, eng))


def isap(x):
    return not isinstance(x, (int, float)) and x is not None


def build_nc():
    nc = bass.Bass("TRN2", target_bir_lowering=False)
    S = Sched(nc)
    stack = ExitStack()

    def din(name, shape, dt=F32):
        return nc.dram_tensor(name, list(shape), dt, kind="ExternalInput").ap()

    def dout(name, shape, dt=F32):
        return nc.dram_tensor(name, list(shape), dt, kind="ExternalOutput").ap()

    def sb(name, shape, dt=F32):
        return stack.enter_context(nc.sbuf_tensor("sb_" + name, list(shape), dt))

    xT_d = din("xT", [D, SEQ])
    xsT_d = din("xsT", [D, NS])
    w_in_d = din("w_in", [L, D, INC])
    w_glu_d = din("w_glu", [L, 1024, 1024])
    w_out_d = din("w_out", [L, D, D])
    w_up_d = din("w_up", [L, D, 2 * DFF])
    w_down_d = din("w_down", [L, DFF, D])
    vec16_d = din("vec16", [128, (2 * L + 1) * 16])
    vec8_d = din("vec8", [128, L * 4 * 8])
    convp_d = din("convp", [L, 128, 4, 88])
    sinks_d = din("sinks", [128, L * 16])
    cosT_d = din("cosT", [128, SEQ])
    sinT_d = din("sinT", [128, SEQ])
    coss_d = din("coss", [128, NS])
    sins_d = din("sins", [128, NS])
    ident_d = din("ident", [128, 128])
    pswap_d = din("pswap", [128, 128])
    maskN_d = din("maskN", [128, 256])
    mask0_d = din("mask0", [128, 256])
    rmask_d = din("rmask", [128, 2])
    bexp_d = din("bexp", [L, 2, 128, 1024])
    pwb_d = din("pwb", [L, 3, 128, 1024])
    cexp_d = din("cexp", [L, 2, 128, 1024])
    pwc_d = din("pwc", [L, 3, 128, 32])
    ckT_d = din("ckT", [L, NS, 128, 128])
    ck_d = din("ck", [L, NS, 128, 128])
    cv_d = din("cv", [L, NS, 128, 128])
    h0_d = din("h0", [L, 128, 2, 32, NS])
    sconv_d = din("sconv", [L, 128, 88, 2, NS])

    yT_o = dout("yT", [D, SEQ])
    ysT_o = dout("ysT", [D, NS])
    kp_o = dout("kp", [L, 128, 128])
    vp_o = dout("vp", [L, 128, 128])
    ssmp_o = dout("ssmp", [L, 128, 2, 32])
    convp_o = dout("convpo", [L, 128, 88, 2])
    ks_o = dout("ks", [L, NS, 128, 128])
    vs_o = dout("vs", [L, NS, 128, 128])
    ssms_o = dout("ssms", [L, 128, 2, 32, NS])
    convs_o = dout("convs", [L, 128, 88, 2, NS])

    PS = stack.enter_context(nc.psum_tensor("ps", [128, 7, 512], F32))
    PSB = stack.enter_context(nc.psum_tensor("psb", [128, 1024], BF16))
    Wb = [sb("w%d" % i, [128, 8192], BF16) for i in range(2)]
    ident_f = sb("ident_f", [128, 128])
    ident_b = sb("ident_b", [128, 128], BF16)
    pswap_f = sb("pswap_f", [128, 128])
    pswap_b = sb("pswap_b", [128, 128], BF16)
    ones_b = sb("ones_b", [128, 128], BF16)
    maskN = sb("maskN", [128, 256])
    mask0 = sb("mask0", [128, 256])
    vec16 = sb("vec16", [128, (2 * L + 1) * 16])
    vec8 = sb("vec8", [128, L * 4 * 8])
    bhalf = sb("bhalf", [128, L * 8])
    sinks = sb("sinks", [128, L * 16])
    negsink = sb("negsink", [128, L * 16])
    eps_t = sb("eps_t", [128, 1])
    convp = sb("convp", [128, 4, 88])
    ZA = sb("ZA", [128, 2, 1024], BF16)
    ZB = sb("ZB", [128, 2, 1024], BF16)
    WCz = sb("WCz", [128, 2, 8, 4, 64], BF16)
    rmask = sb("rmask", [128, 2])
    wsc = nc.dram_tensor("wsc", [L, 128, 2, 2048], BF16, kind="Internal").ap()
    S.track.add("wsc")
    Aco = sb("Aco", [128, L, 2, 32])
    Bco = sb("Bco", [128, L, 2, 32])
    kprev = sb("kprev", [128, L, 128], BF16)
    vprev = sb("vprev", [128, L, 128], BF16)
    Scar = sb("Scar", [128, L, 2, 32])
    tails = sb("tails", [128, L, 88, 2])
    sqs = [sb("sq%d" % i, [128, NT], BF16) for i in range(2)]
    stdt = sb("stdt", [128, NT])
    rstd = sb("rstd", [128, NT])
    ropeA = sb("ropeA", [128, NT], BF16)
    ropeB = sb("ropeB", [128, NT], BF16)
    st_rmax = sb("st_rmax", [128, 16])
    st_nb = sb("st_nb", [128, 16])
    st_rsum = sb("st_rsum", [128, 16])
    st_es = sb("st_es", [128, 16])
    st_den = sb("st_den", [128, 16])
    st_rden = sb("st_rden", [128, 16])
    arena = sb("arena", [128, 10304])
    Xs = arena[:, 0:4096].rearrange("p (r g k) -> p r g k", r=2, g=32)
    Sh = arena[:, 4096:8256].rearrange("p (k r g) -> p k r g", r=2, g=32)
    Shb = arena[:, 8256:10304].bitcast(BF16).rearrange("p (r g k) -> p r g k", r=2, g=32)
    actT = arena[:, 0:5632].bitcast(BF16).rearrange("p (i n) -> p i n", n=NT)
    setA = arena[:, 0:6144].rearrange("p (a w) -> p a w", w=512)
    WBc = arena[:, 8256:9280].bitcast(BF16).rearrange("p (r c) -> p r c", r=2)
    WCc = arena[:, 9280:10304].bitcast(BF16).rearrange("p (r c) -> p r c", r=2)
    Sm = [arena[:, i * 256:(i + 1) * 256] for i in range(2)]
    Pb = [arena[:, 512 + i * 128:512 + (i + 1) * 128].bitcast(BF16) for i in range(2)]
    PTs = [arena[:, 768 + i * 128:768 + (i + 1) * 128].bitcast(BF16) for i in range(2)]
    Otok = arena[:, 1024:1536].bitcast(BF16)
    stg = arena[:, 6144:7168].bitcast(BF16).rearrange("p (a w) -> p a w", w=512)
    extg = arena[:, 5632:5632 + NT + 2]
    extv = arena[:, 6152:6152 + NT + 2]
    cg = arena[:, 6672:6672 + NT]
    cvv = arena[:, 7184:7184 + NT]
    sc_t1 = sb("sc_t1", [128, 2, 32])
    sc_t2 = sb("sc_t2", [128, 2, 32])
    tmpa = sb("tmpa", [128, NT])
    tmpb = sb("tmpb", [128, NT])

    class Grp:
        pass

    def mkgrp(name, N, nunits):
        g = Grp()
        g.name = name
        g.N = N
        g.xT = sb(name + "_xT", [128, NCH, N])
        g.hT = sb(name + "_hT", [128, NCH, N], BF16)
        g.qT = sb(name + "_qT", [128, 8, N], BF16)
        g.uT = sb(name + "_uT", [128, 8, N], BF16)
        g.yT = g.qT
        g.krot = sb(name + "_krot", [128, N])
        g.vTf = sb(name + "_vTf", [128, N])
        g.vTb = sb(name + "_vTb", [128, N], BF16)
        g.cos = sb(name + "_cos", [128, N])
        g.sin = sb(name + "_sin", [128, N])
        return g

    GP = mkgrp("p", NT, 1)
    GS = mkgrp("s", NS, NS)
    GP.kT = sb("p_kT", [128, 128 + NT], BF16)
    GP.Vt = sb("p_Vt", [128, 5, 128], BF16)
    GS.kT = sb("s_kT", [128, NS, 256], BF16)
    GS.Vt = sb("s_Vt", [128, NS, 2, 128], BF16)
    GS.h0 = sb("s_h0", [128, 2, 32, NS])
    GS.sconv = sb("s_sconv", [128, 88, 2, NS])
    GS.snew = sb("s_snew", [128, 2, 32, NS])
    GS.snb = sb("s_snb", [128, 2, 32, NS], BF16)
    GS.cout = sb("s_cout", [128, 88, 2, NS])
    GS.Xs = sb("s_Xs", [128, 2, 32, NS])
    GS.yT_act = sb("s_yTact", [128, 22, NS], BF16)

    def MM(out, lhsT, rhs, start=True, stop=True):
        S.add("pe", lambda e: e.matmul(out, lhsT=lhsT, rhs=rhs, start=start, stop=stop), [lhsT, rhs], [out])

    def TR(out, in_, ident):
        S.add("pe", lambda e: e.transpose(out, in_, ident), [in_, ident], [out])

    def ACT(out, in_, func, bias=None, scale=None, accum=None):
        kw = {}
        rd = [in_]
        wr = [out]
        if bias is not None:
            kw["bias"] = bias
            if isap(bias):
                rd.append(bias)
        if scale is not None:
            kw["scale"] = scale
            if isap(scale):
                rd.append(scale)
        if accum is not None:
            kw["accum_out"] = accum
            wr.append(accum)
        S.add("act", lambda e: e.activation(out=out, in_=in_, func=func, **kw), rd, wr)

    def TT(out, a, b, op, eng="dve"):
        S.add(eng, lambda e: e.tensor_tensor(out=out, in0=a, in1=b, op=op), [a, b], [out])

    def TS(out, a, s1, s2, op0, op1=None, eng="dve"):
        rd = [a] + [x for x in (s1, s2) if isap(x)]
        if op1 is None:
            S.add(eng, lambda e: e.tensor_scalar(out=out, in0=a, scalar1=s1, scalar2=None, op0=op0), rd, [out])
        else:
            S.add(eng, lambda e: e.tensor_scalar(out=out, in0=a, scalar1=s1, scalar2=s2, op0=op0, op1=op1), rd, [out])

    def STT(out, a, sc, b, op0, op1):
        rd = [a, b] + ([sc] if isap(sc) else [])
        S.add("dve", lambda e: e.scalar_tensor_tensor(out=out, in0=a, scalar=sc, in1=b, op0=op0, op1=op1), rd, [out])

    def CP(out, in_, eng="dve"):
        if eng == "act":
            ACT(out, in_, AF.Copy)
        else:
            S.add(eng, lambda e: e.tensor_copy(out=out, in_=in_), [in_], [out])

    def RECIP(out, in_):
        S.add("dve", lambda e: e.reciprocal(out=out, in_=in_), [in_], [out])

    def MEMSET(ap, val):
        S.add("dve", lambda e: e.memset(ap, val), [], [ap])

    def DMA(out, in_, q="sp"):
        S.add(q, lambda e: e.dma_start(out=out, in_=in_), [in_], [out], dma=True)

    def REDMAX(out, in_):
        S.add("dve", lambda e: e.tensor_reduce(out=out, in_=in_, axis=mybir.AxisListType.X, op=ALU.max), [in_], [out])

    def TTR(out, a, b, op0, op1, init, accum):
        S.add("dve", lambda e: e.tensor_tensor_reduce(out=out, in0=a, in1=b, scale=1.0, scalar=init, op0=op0, op1=op1,
                                                     accum_out=accum), [a, b], [out, accum])

    DMA(ident_f[:], ident_d)
    DMA(pswap_f[:], pswap_d)
    DMA(maskN[:], maskN_d)
    DMA(mask0[:], mask0_d)
    DMA(vec16[:], vec16_d)
    DMA(vec8[:], vec8_d)
    DMA(sinks[:], sinks_d)
    DMA(rmask[:], rmask_d)
    CP(ident_b[:], ident_f[:])
    CP(pswap_b[:], pswap_f[:])
    MEMSET(ones_b[:], 1.0)
    MEMSET(eps_t[:], 1e-6)
    TS(negsink[:], sinks[:], -1.0, None, ALU.mult)
    MEMSET(kprev[:], 0.0)
    MEMSET(vprev[:], 0.0)
    MEMSET(Scar[:], 0.0)
    MEMSET(tails[:], 0.0)
    MEMSET(GS.kT[:], 0.0)
    MEMSET(GS.Vt[:], 0.0)
    MEMSET(WCz[:], 0.0)
    for l in range(L):
        TS(bhalf[:, l * 8:(l + 1) * 8], vec8[:, (l * 4 + 3) * 8:(l * 4 + 4) * 8], 0.5, None, ALU.mult)

    def v16(l, which):
        o = (l * 2 + which) * 16
        return vec16[:, o:o + 16]

    def v8(l, which):
        o = (l * 4 + which) * 8
        return vec8[:, o:o + 8]

    TWO_PI = 2.0 * math.pi

    def sincos(dst, theta, shift, W, tmp1, tmp2i, tmp3):
        TS(tmp1, theta, 1.0 / TWO_PI, shift, ALU.mult, ALU.add)
        CP(tmp2i, tmp1)
        CP(tmp3, tmp2i)
        TT(tmp1, tmp1, tmp3, ALU.subtract)
        TS(tmp3, tmp1, 0.5, None, ALU.is_gt)
        TT(tmp1, tmp1, tmp3, ALU.subtract)
        TS(tmp3, tmp1, -0.5, None, ALU.is_lt)
        TT(tmp1, tmp1, tmp3, ALU.add)
        ACT(dst, tmp1, AF.Sin, scale=6.283185)

    def lam_q(W, ar, ai, ldt, lr, li, qr, qi, t1, t2i, t3, t4):
        ACT(ldt, ldt, AF.Exp)
        TT(t4, ai, ldt, ALU.mult)
        sincos(li, t4, 0.0, W, t1, t2i, t3)
        sincos(lr, t4, 0.25, W, t1, t2i, t3)
        TT(t4, ar, ldt, ALU.mult)
        ACT(t4, t4, AF.Exp)
        TT(lr, lr, t4, ALU.mult)
        TT(li, li, t4, ALU.mult)
        TT(t1, ar, ar, ALU.mult)
        TT(t3, ai, ai, ALU.mult)
        TT(t1, t1, t3, ALU.add)
        RECIP(t1, t1)
        TS(t4, lr, -1.0, None, ALU.add)
        TT(qr, t4, ar, ALU.mult)
        TT(t3, li, ai, ALU.mult)
        TT(qr, qr, t3, ALU.add)
        TT(qr, qr, t1, ALU.mult)
        TT(qi, li, ar, ALU.mult)
        TT(t3, t4, ai, ALU.mult)
        TT(qi, qi, t3, ALU.subtract)
        TT(qi, qi, t1, ALU.mult)

    for l in range(L):
        for hv in range(2):
            hs = slice(hv * 512, (hv + 1) * 512)
            ar, ai, ldt = setA[:, 0, :], setA[:, 1, :], setA[:, 2, :]
            lr, li, qr, qi = setA[:, 3, :], setA[:, 4, :], setA[:, 5, :], setA[:, 6, :]
            t1, t3, t4 = setA[:, 7, :], setA[:, 8, :], setA[:, 9, :]
            t2i = setA[:, 10, :].bitcast(I32)
            bre, bim = setA[:, 10, :], setA[:, 11, :]
            DMA(ar, pwb_d[l, 0, :, hs])
            DMA(ai, pwb_d[l, 1, :, hs])
            DMA(ldt, pwb_d[l, 2, :, hs])
            lam_q(512, ar, ai, ldt, lr, li, qr, qi, t1, t2i, t3, t4)
            DMA(bre, bexp_d[l, 0, :, hs])
            DMA(bim, bexp_d[l, 1, :, hs])
            TT(t1, qr, bre, ALU.mult)
            TT(t3, qi, bim, ALU.mult)
            TT(stg[:, 0, :], t1, t3, ALU.subtract)
            TT(t1, qr, bim, ALU.mult)
            TT(t3, qi, bre, ALU.mult)
            TT(stg[:, 1, :], t1, t3, ALU.add)
            cre, cim = setA[:, 10, :], setA[:, 11, :]
            DMA(cre, cexp_d[l, 0, :, hs])
            DMA(cim, cexp_d[l, 1, :, hs])
            CP(stg[:, 2, :], cre)
            TS(stg[:, 3, :], cim, -1.0, None, ALU.mult)
            DMA(wsc[l, :, 0, hv * 512:(hv + 1) * 512], stg[:, 0, :])
            DMA(wsc[l, :, 0, 1024 + hv * 512:1024 + (hv + 1) * 512], stg[:, 1, :])
            DMA(wsc[l, :, 1, hv * 512:(hv + 1) * 512], stg[:, 2, :])
            DMA(wsc[l, :, 1, 1024 + hv * 512:1024 + (hv + 1) * 512], stg[:, 3, :])
        sar, sai, sdt = setA[:, 0, 0:32], setA[:, 1, 0:32], setA[:, 2, 0:32]
        slr, sli, sqr, sqi = setA[:, 3, 0:32], setA[:, 4, 0:32], setA[:, 5, 0:32], setA[:, 6, 0:32]
        s1, s3, s4 = setA[:, 7, 0:32], setA[:, 8, 0:32], setA[:, 9, 0:32]
        s2i = setA[:, 10, 0:32].bitcast(I32)
        DMA(sar, pwc_d[l, 0])
        DMA(sai, pwc_d[l, 1])
        DMA(sdt, pwc_d[l, 2])
        lam_q(32, sar, sai, sdt, slr, sli, sqr, sqi, s1, s2i, s3, s4)
        CP(Aco[:, l, 0, :], slr)
        CP(Aco[:, l, 1, :], slr)
        TS(Bco[:, l, 0, :], sli, -1.0, None, ALU.mult)
        CP(Bco[:, l, 1, :], sli)

    state = dict(wi=0, bank=0)

    def wtile(view, KC, c0, cw, parts=None):
        buf = Wb[state["wi"] % 2]
        state["wi"] += 1
        t = buf[:, 0:KC * cw].rearrange("p (k c) -> p k c", c=cw)
        if parts is None:
            DMA(t, view[:, :, c0:c0 + cw], q="pool")
        else:
            o = 0
            for (v2, a0, aw) in parts:
                DMA(t[:, :, o:o + aw], v2[:, :, a0:a0 + aw], q="pool")
                o += aw
        return t

    def nextbank():
        b = state["bank"]
        state["bank"] = (b + 1) % 4
        return b

    def linear(groups, view, KC, ncols, rhs_of, consume, cwmax=512):
        c0 = 0
        while c0 < ncols:
            cw = min(cwmax, ncols - c0)
            t = wtile(view, KC, c0, cw)
            for m in range(cw // 128):
                mt = c0 // 128 + m
                if MAXMT is not None and mt >= MAXMT:
                    raise _Stop()
                for g in groups:
                    ps = PS[:, nextbank(), 0:g.N]
                    for kc in range(KC):
                        MM(ps, t[:, kc, m * 128:(m + 1) * 128], rhs_of(g, kc), start=(kc == 0), stop=(kc == KC - 1))
                    consume(g, mt, ps)
            c0 += cw

    def rmsnorm(g, src, nch, gain, dst, Dn):
        N = g.N
        pst = PS[:, 4, 0:N]
        for c in range(nch):
            sq = sqs[c % 2][:, 0:N]
            ACT(sq, src[:, c, :], AF.Square)
            MM(pst, ones_b[:], sq, start=(c == 0), stop=(c == nch - 1))
        ACT(stdt[:, 0:N], pst, AF.Sqrt, bias=eps_t[:, 0:1], scale=1.0 / Dn)
        RECIP(rstd[:, 0:N], stdt[:, 0:N])
        for c in range(nch):
            STT(dst[:, c, :], src[:, c, :], gain[:, c:c + 1], rstd[:, 0:N], ALU.mult, ALU.mult)

    def rope(g, ps, dst_bf, dst_f32=None):
        N = g.N
        TT(ropeA[:, 0:N], ps, g.cos[:, 0:N], ALU.mult)
        TT(ropeB[:, 0:N], ps, g.sin[:, 0:N], ALU.mult)
        ps2 = PS[:, 5, 0:N]
        MM(ps2, ident_b[:], ropeA[:, 0:N], start=True, stop=False)
        MM(ps2, pswap_b[:], ropeB[:, 0:N], start=False, stop=True)
        ACT(dst_bf, ps2, AF.Copy)
        if dst_f32 is not None:
            import os
            kv = os.environ.get("KVAR", "a")
            if kv == "a":
                ACT(dst_f32, ps2, AF.Copy)
            elif kv == "b":
                CP(dst_f32, ps2)
            else:
                pass

    def attn_unit(l, nq, qap_of, kT2, Vblk, mask, acols, g):
        for m in range(8):
            for e in range(2):
                hi = m * 2 + e
                col = l * 16 + hi
                Sps = PS[0:nq, 6 if e == 0 else 4, 0:256]
                MM(Sps, qap_of(m, e), kT2[e * 64:(e + 1) * 64, :])
                astep()
                TT(Sm[e][0:nq, :], Sps, mask[0:nq, :], ALU.add)
                astep()
                REDMAX(st_rmax[0:nq, hi:hi + 1], Sm[e][0:nq, :])
                astep()
                TS(st_nb[0:nq, hi:hi + 1], st_rmax[0:nq, hi:hi + 1], -0.125, negsink[0:nq, col:col + 1], ALU.mult, ALU.min)
                astep()
                ACT(Pb[e][0:nq, :], Sm[e][0:nq, :], AF.Exp, bias=st_nb[0:nq, hi:hi + 1], scale=0.125,
                    accum=st_rsum[0:nq, hi:hi + 1])
                astep()
                ACT(st_es[0:nq, hi:hi + 1], sinks[0:nq, col:col + 1], AF.Exp, bias=st_nb[0:nq, hi:hi + 1], scale=1.0)
                astep()
                TT(st_den[0:nq, hi:hi + 1], st_rsum[0:nq, hi:hi + 1], st_es[0:nq, hi:hi + 1], ALU.add)
                astep()
                RECIP(st_rden[0:nq, hi:hi + 1], st_den[0:nq, hi:hi + 1])
                astep()
                for kb in range(2):
                    TR(PSB[:, e * 256 + kb * 128:e * 256 + kb * 128 + nq], Pb[e][0:nq, kb * 128:(kb + 1) * 128],
                       ident_b[0:nq, 0:nq])
                    astep()
                for kb in range(2):
                    CP(PTs[e][:, kb * 128:kb * 128 + nq], PSB[:, e * 256 + kb * 128:e * 256 + kb * 128 + nq])
                    astep()
                Ops = PS[0:nq, 5, (hi % 8) * 64:(hi % 8 + 1) * 64]
                for kb in range(2):
                    MM(Ops, PTs[e][:, kb * 128:kb * 128 + nq], Vblk[kb][:, e * 64:(e + 1) * 64], start=(kb == 0), stop=(kb == 1))
                    astep()
                ACT(Otok[0:nq, hi * 64:(hi + 1) * 64], Ops, AF.Copy, scale=st_rden[0:nq, hi:hi + 1])
                astep()
        for m in range(8):
            o = 512 + (m % 4) * 128
            TR(PSB[:, o:o + nq], Otok[0:nq, m * 128:(m + 1) * 128], ident_b[0:nq, 0:nq])
            astep()
            CP(g.hT[:, m, acols], PSB[:, o:o + nq])
            astep()

    def gelu_inplace(g):
        N = g.N
        for c in range(8):
            y = g.yT[:, c, :]
            TT(tmpa[:, 0:N], y, y, ALU.mult)
            TS(tmpa[:, 0:N], tmpa[:, 0:N], 0.044715, 1.0, ALU.mult, ALU.add)
            TT(tmpa[:, 0:N], tmpa[:, 0:N], y, ALU.mult)
            ACT(tmpb[:, 0:N], tmpa[:, 0:N], AF.Tanh, scale=0.7978845608028654)
            TS(tmpb[:, 0:N], tmpb[:, 0:N], 0.5, 0.5, ALU.mult, ALU.add)
            TT(y, tmpb[:, 0:N], y, ALU.mult)

    xT_v = xT_d.rearrange("(c p) t -> p c t", p=128)
    yT_v = yT_o.rearrange("(c p) t -> p c t", p=128)
    xsT_v = xsT_d.rearrange("(c p) t -> p c t", p=128)
    ysT_v = ysT_o.rearrange("(c p) t -> p c t", p=128)

    def chk(j, l, p):
        if LIMIT is not None and (j, l, p) >= tuple(LIMIT):
            raise _Stop()

    def main_loop():
      for j in range(NTILES):
          t0 = j * NT
          groups = [GP] + ([GS] if (j == 0 and not NOSAMP) else [])
          chk(j, -1, 0)
          for c4 in range(4):
              DMA(GP.xT[:, c4 * 4:(c4 + 1) * 4, :], xT_v[:, c4 * 4:(c4 + 1) * 4, t0:t0 + NT])
          DMA(GP.cos[:], cosT_d[:, t0:t0 + NT])
          DMA(GP.sin[:], sinT_d[:, t0:t0 + NT])
          if j == 0 and not NOSAMP:
              DMA(GS.xT[:], xsT_v)
              DMA(GS.cos[:], coss_d)
              DMA(GS.sin[:], sins_d)
          for l in range(L):
              last = (j == NTILES - 1)
              chk(j, l, 0)
              for g in groups:
                  rmsnorm(g, g.xT, NCH, v16(l, 0), g.hT, float(D))
              chk(j, l, 0.1)
              CP(GP.kT[:, 0:128], kprev[:, l, :])
              CP(GP.Vt[:, 0, :], vprev[:, l, :])
              if j == 0 and not NOSAMP:
                  for b in range(NS):
                      DMA(GS.kT[:, b, 0:128], ckT_d[l, b], q="pool")
                      DMA(GS.Vt[:, b, 0, :], cv_d[l, b], q="pool")
                  chk(j, l, 0.12)
                  DMA(GS.h0[:], h0_d[l])
                  DMA(GS.sconv[:], sconv_d[l])
              DMA(convp[:], convp_d[l])
              chk(j, l, 0.13)
              DMA(WBc.rearrange("p r c -> p (r c)"), wsc[l, :, 0, :])
              DMA(WCc.rearrange("p r c -> p (r c)"), wsc[l, :, 1, :])
              for r_ in range(2):
                  TS(ZA[:, r_, :], WBc[:, r_, :], rmask[:, 0:1], None, ALU.mult)
                  TS(ZB[:, r_, :], WBc[:, r_, :], rmask[:, 1:2], None, ALU.mult)
              WCv = WCc.rearrange("p r (f q c) -> p r f q c", f=8, q=4)
              for r_ in range(2):
                  for q_ in range(4):
                      CP(WCz[:, r_, :, q_, (q_ % 2) * 32:(q_ % 2) * 32 + 32], WCv[:, r_, :, q_, :])

              chk(j, l, 0.2)
              def cons_in(g, mt, ps):
                  N = g.N
                  if mt < 8:
                      rope(g, ps, g.qT[:, mt, :])
                  elif mt == 8:
                      if g is GP:
                          rope(g, ps, g.kT[:, 128:128 + N], g.krot[:, 0:N])
                      else:
                          rope(g, ps, g.kT[:, :, 128:129].rearrange("p b o -> p (b o)"), g.krot[:, 0:N])
                  elif mt == 9:
                      ACT(g.vTb[:, 0:N], ps, AF.Copy)
                      ACT(g.vTf[:, 0:N], ps, AF.Copy)
                  else:
                      ACT(g.uT[:, mt - 10, :], ps, AF.Copy)

              w_in_v = w_in_d[l].rearrange("(k p) c -> p k c", p=128)
              linear(groups, w_in_v, NCH, INC, lambda g, kc: g.hT[:, kc, :], cons_in)

              chk(j, l, 0.3)
              for blk in range(4):
                  o = (blk % 4) * 128
                  TR(PSB[:, 512 + o:512 + o + 128], GP.vTb[:, blk * 128:(blk + 1) * 128], ident_b[:])
                  CP(GP.Vt[:, blk + 1, :], PSB[:, 512 + o:512 + o + 128])
              chk(j, l, 0.4)
              CP(kprev[:, l, :], GP.kT[:, NT:NT + 128])
              CP(vprev[:, l, :], GP.Vt[:, 4, :])
              if last:
                  DMA(kp_o[l], GP.krot[:, NT - 128:NT])
                  DMA(vp_o[l], GP.vTf[:, NT - 128:NT])
              if j == 0 and not NOSAMP:
                  for b in range(NS):
                      MM(PS[0:1, 4, 0:128], GS.vTb[:, b:b + 1], ident_b[:])
                      CP(GS.Vt[0:1, b, 1, :], PS[0:1, 4, 0:128])
                      DMA(ks_o[l, b, 0:127, :], ck_d[l, b, 1:128, :])
                      DMA(vs_o[l, b, 0:127, :], cv_d[l, b, 1:128, :])
                      DMA(ks_o[l, b, 127, :].rearrange("(p o) -> p o", o=1), GS.krot[:, b:b + 1])
                      DMA(vs_o[l, b, 127, :].rearrange("(p o) -> p o", o=1), GS.vTf[:, b:b + 1])

              chk(j, l, 1)
              for qb in range(4):
                  mask = mask0 if (j == 0 and qb == 0) else maskN
                  attn_unit(l, 128,
                            lambda m, e, qb=qb: GP.qT[e * 64:(e + 1) * 64, m, qb * 128:(qb + 1) * 128],
                            GP.kT[:, qb * 128:qb * 128 + 256],
                            [GP.Vt[:, qb, :], GP.Vt[:, qb + 1, :]],
                            mask, slice(qb * 128, (qb + 1) * 128), GP)
              if j == 0 and not NOSAMP:
                  for b in range(NS):
                      attn_unit(l, 1,
                                lambda m, e, b=b: GS.qT[e * 64:(e + 1) * 64, m, b:b + 1],
                                GS.kT[:, b, :],
                                [GS.Vt[:, b, 0, :], GS.Vt[:, b, 1, :]],
                                maskN, slice(b, b + 1), GS)

              chk(j, l, 2)
              CP(Sh[:, 0, :, :], Scar[:, l, :, :])
              for stt in range(8):
                  cs = slice(stt * 64, (stt + 1) * 64)
                  Xv = Xs.rearrange("p r (t q) k -> p r t (q k)", q=4)
                  for ri in range(2):
                      for bb in range(2):
                          for hh in range(2):
                              bank = 2 * hh + bb
                              for i8 in range(8):
                                  t = bb * 4 + i8 // 2
                                  q4 = 2 * hh + i8 % 2
                                  Z = ZA if q4 % 2 == 0 else ZB
                                  MM(PS[:, bank, i8 * 64:(i8 + 1) * 64],
                                     Z[hh * 64:(hh + 1) * 64, ri, t * 128:(t + 1) * 128],
                                     GP.uT[hh * 64:(hh + 1) * 64, t, cs])
                          for hh in range(2):
                              bank = 2 * hh + bb
                              ACT(Xv[:, ri, bb * 4:(bb + 1) * 4, hh * 128:(hh + 1) * 128],
                                  PS[:, bank, :].rearrange("p (a b) -> p a b", b=128), AF.Copy)
                  if stt == 0:
                      chk(j, l, 2.1)
                  if stt > 0:
                      CP(Sh[:, 0, :, :], Sh[:, 64, :, :])
                  for k in range(64):
                      TT(sc_t1[:], Aco[:, l, :, :], Sh[:, k, :, :], ALU.mult)
                      TT(sc_t2[:, 0, :], Bco[:, l, 0, :], Sh[:, k, 1, :], ALU.mult)
                      TT(sc_t2[:, 1, :], Bco[:, l, 1, :], Sh[:, k, 0, :], ALU.mult)
                      TT(sc_t1[:], sc_t1[:], sc_t2[:], ALU.add)
                      TT(Sh[:, k + 1, :, :], sc_t1[:], Xs[:, :, :, k], ALU.add)
                  if stt == 0:
                      chk(j, l, 2.2)
                  for r_ in range(2):
                      CP(Shb[:, r_, :, :], Sh[:, 1:65, r_, :].rearrange("p k g -> p g k"))
                  if stt == 0:
                      chk(j, l, 2.3)
                  for ft in range(8):
                      if ft % 8 == 0:
                          bank = nextbank()
                      yps = PS[:, bank, (ft % 8) * 64:(ft % 8 + 1) * 64]
                      for q4 in range(4):
                          gp = ft * 4 + q4
                          for ri in range(2):
                              hh = q4 // 2
                              MM(yps[hh * 64:(hh + 1) * 64, :], WCz[:, ri, ft, q4, :], Shb[:, ri, gp, :],
                                 start=(q4 % 2 == 0 and ri == 0), stop=(q4 % 2 == 1 and ri == 1))
                  for ft in range(8):
                      yps = PS[:, bank, (ft % 8) * 64:(ft % 8 + 1) * 64]
                      STT(GP.yT[:, ft, cs], GP.uT[:, ft, cs], v8(l, 2)[:, ft:ft + 1], yps, ALU.mult, ALU.add)
                  if stt == 0:
                      chk(j, l, 2.4)
              CP(Scar[:, l, :, :], Sh[:, 64, :, :])
              chk(j, l, 2.5)
              if last:
                  DMA(ssmp_o[l], Sh[:, 64, :, :])
              if j == 0 and not NOSAMP:
                  g = GS
                  Xvs = g.Xs[:].rearrange("p r (t q) k -> p r t (q k)", q=4)
                  for ri in range(2):
                      for bb in range(2):
                          for hh in range(2):
                              bank = 2 * hh + bb
                              for i8 in range(8):
                                  t = bb * 4 + i8 // 2
                                  q4 = 2 * hh + i8 % 2
                                  Z = ZA if q4 % 2 == 0 else ZB
                                  MM(PS[:, bank, i8 * NS:(i8 + 1) * NS],
                                     Z[hh * 64:(hh + 1) * 64, ri, t * 128:(t + 1) * 128],
                                     g.uT[hh * 64:(hh + 1) * 64, t, :])
                          for hh in range(2):
                              bank = 2 * hh + bb
                              CP(Xvs[:, ri, bb * 4:(bb + 1) * 4, hh * 2 * NS:(hh + 1) * 2 * NS],
                                 PS[:, bank, 0:8 * NS].rearrange("p (a b) -> p a b", b=2 * NS))
                  for b in range(NS):
                      TT(sc_t1[:], Aco[:, l, :, :], g.h0[:, :, :, b], ALU.mult)
                      TT(sc_t2[:, 0, :], Bco[:, l, 0, :], g.h0[:, 1, :, b], ALU.mult)
                      TT(sc_t2[:, 1, :], Bco[:, l, 1, :], g.h0[:, 0, :, b], ALU.mult)
                      TT(sc_t1[:], sc_t1[:], sc_t2[:], ALU.add)
                      TT(g.snew[:, :, :, b], sc_t1[:], g.Xs[:, :, :, b], ALU.add)
                  CP(g.snb[:], g.snew[:])
                  DMA(ssms_o[l], g.snew[:])
                  bank = nextbank()
                  for ft in range(8):
                      yps = PS[:, bank, ft * NS:(ft + 1) * NS]
                      for q4 in range(4):
                          gp = ft * 4 + q4
                          for ri in range(2):
                              hh = q4 // 2
                              MM(yps[hh * 64:(hh + 1) * 64, :], WCz[:, ri, ft, q4, :], g.snb[:, ri, gp, :],
                                 start=(q4 % 2 == 0 and ri == 0), stop=(q4 % 2 == 1 and ri == 1))
                  for ft in range(8):
                      yps = PS[:, bank, ft * NS:(ft + 1) * NS]
                      STT(g.yT[:, ft, :], g.uT[:, ft, :], v8(l, 2)[:, ft:ft + 1], yps, ALU.mult, ALU.add)

              chk(j, l, 3)
              for g in groups:
                  gelu_inplace(g)

              def cons_glu(g, mt, ps):
                  N = g.N
                  ACT(tmpa[:, 0:N], ps, AF.Tanh, bias=bhalf[:, l * 8 + mt:l * 8 + mt + 1], scale=0.5)
                  TS(tmpa[:, 0:N], tmpa[:, 0:N], 0.5, 0.5, ALU.mult, ALU.add)
                  TT(g.hT[:, 8 + mt, :], tmpa[:, 0:N], g.yT[:, mt, :], ALU.mult)

              w_glu_v = w_glu_d[l].rearrange("(k p) c -> p k c", p=128)
              linear(groups, w_glu_v, 8, 1024, lambda g, kc: g.yT[:, kc, :], cons_glu)

              chk(j, l, 4)
              for g in groups:
                  rmsnorm(g, g.hT[:, 0:8, :], 8, v8(l, 0), g.hT[:, 0:8, :], 1024.0)
                  rmsnorm(g, g.hT[:, 8:16, :], 8, v8(l, 1), g.hT[:, 8:16, :], 1024.0)

              def cons_res(g, mt, ps):
                  TT(g.xT[:, mt, :], ps, g.xT[:, mt, :], ALU.add)

              w_out_v = w_out_d[l].rearrange("(k p) c -> p k c", p=128)
              linear(groups, w_out_v, NCH, D, lambda g, kc: g.hT[:, kc, :], cons_res)

              chk(j, l, 5)
              for g in groups:
                  rmsnorm(g, g.xT, NCH, v16(l, 1), g.hT, float(D))
              w_up_v = w_up_d[l].rearrange("(k p) c -> p k c", p=128)
              w_dn_v = w_down_d[l].rearrange("(k p) c -> p k c", p=128)

              def conv_tile(g, i, which, ps, ext, dst):
                  N = g.N
                  ti = i + which * 44
                  if g is GP:
                      CP(ext[:, 0:2], tails[:, l, ti, :])
                      ACT(ext[:, 2:2 + N], ps, AF.Copy)
                      CP(tails[:, l, ti, :], ext[:, N:N + 2])
                      x0, x1, x2 = ext[:, 0:N], ext[:, 1:N + 1], ext[:, 2:N + 2]
                  else:
                      ACT(ext[:, 0:N], ps, AF.Copy)
                      CP(g.cout[:, ti, 0, :], g.sconv[:, ti, 1, :])
                      CP(g.cout[:, ti, 1, :], ext[:, 0:N])
                      x0, x1, x2 = g.sconv[:, ti, 0, :], g.sconv[:, ti, 1, :], ext[:, 0:N]
                  TS(dst[:, 0:N], x0, convp[:, 0, ti:ti + 1], convp[:, 3, ti:ti + 1], ALU.mult, ALU.add)
                  STT(dst[:, 0:N], x1, convp[:, 1, ti:ti + 1], dst[:, 0:N], ALU.mult, ALU.add)
                  STT(dst[:, 0:N], x2, convp[:, 2, ti:ti + 1], dst[:, 0:N], ALU.mult, ALU.add)

              for hf in range(2):
                  for i2 in range(11):
                      i0 = hf * 22 + i2 * 2
                      t = wtile(None, NCH, 0, 512, parts=[(w_up_v, i0 * 128, 256), (w_up_v, DFF + i0 * 128, 256)])
                      for m in range(2):
                          i = i0 + m
                          for g in groups:
                              N = g.N
                              psg = PS[:, nextbank(), 0:N]
                              for kc in range(NCH):
                                  MM(psg, t[:, kc, m * 128:(m + 1) * 128], g.hT[:, kc, :], start=(kc == 0), stop=(kc == NCH - 1))
                              psv = PS[:, nextbank(), 0:N]
                              for kc in range(NCH):
                                  MM(psv, t[:, kc, 256 + m * 128:256 + (m + 1) * 128], g.hT[:, kc, :], start=(kc == 0),
                                     stop=(kc == NCH - 1))
                              conv_tile(g, i, 0, psg, extg, cg)
                              conv_tile(g, i, 1, psv, extv, cvv)
                              ACT(tmpa[:, 0:N], cg[:, 0:N], AF.Tanh, scale=0.5)
                              TS(tmpa[:, 0:N], tmpa[:, 0:N], 0.5, 0.5, ALU.mult, ALU.add)
                              TT(tmpa[:, 0:N], tmpa[:, 0:N], cg[:, 0:N], ALU.mult)
                              dsta = actT[:, i - hf * 22, 0:N] if g is GP else g.yT_act[:, i - hf * 22, :]
                              TT(dsta, tmpa[:, 0:N], cvv[:, 0:N], ALU.mult)
                  dn_view = w_dn_v[:, hf * 22:(hf + 1) * 22, :]
                  linear(groups, dn_view, 22, D,
                         lambda g, kc: (actT[:, kc, :] if g is GP else g.yT_act[:, kc, :]), cons_res, cwmax=256)
              if last:
                  DMA(convp_o[l], tails[:, l, :, :])
              if j == 0 and not NOSAMP:
                  DMA(convs_o[l], GS.cout[:])
          chk(j, L, 0)
          for g in groups:
              N = g.N
              pst = PS[:, 4, 0:N]
              for c in range(NCH):
                  sq = sqs[c % 2][:, 0:N]
                  ACT(sq, g.xT[:, c, :], AF.Square)
                  MM(pst, ones_b[:], sq, start=(c == 0), stop=(c == NCH - 1))
              ACT(stdt[:, 0:N], pst, AF.Sqrt, bias=eps_t[:, 0:1], scale=1.0 / D)
              RECIP(rstd[:, 0:N], stdt[:, 0:N])
              for c in range(NCH):
                  ob = tmpa if c % 2 == 0 else tmpb
                  STT(ob[:, 0:N], g.xT[:, c, :], v16(L, 0)[:, c:c + 1], rstd[:, 0:N], ALU.mult, ALU.mult)
                  if g is GP:
                      DMA(yT_v[:, c, t0:t0 + NT], ob[:, 0:N])
                  else:
                      DMA(ysT_v[:, c, :], ob[:, 0:N])

    try:
        main_loop()
    except _Stop:
        pass
    if DBG:
        def dump(name, ap2d, dt):
            shp = [int(x) for x in ap2d.shape]
            d = nc.dram_tensor("dbg_" + name, shp, dt, kind="ExternalOutput").ap()
            DMA(d, ap2d)
        for gname, g in (("p", GP), ("s", GS)):
            dump(gname + "_xT", g.xT[:].rearrange("p c n -> p (c n)"), F32)
            dump(gname + "_hT", g.hT[:].rearrange("p c n -> p (c n)"), BF16)
            dump(gname + "_qT", g.qT[:].rearrange("p c n -> p (c n)"), BF16)
            dump(gname + "_uT", g.uT[:].rearrange("p c n -> p (c n)"), BF16)
            dump(gname + "_krot", g.krot[:], F32)
            dump(gname + "_vTf", g.vTf[:], F32)
        dump("p_kT", GP.kT[:], BF16)
        dump("p_Vt", GP.Vt[:].rearrange("p a b -> p (a b)"), BF16)
        dump("s_kT", GS.kT[:].rearrange("p a b -> p (a b)"), BF16)
        dump("s_Vt", GS.Vt[:].rearrange("p a b c -> p (a b c)"), BF16)
        dump("arena", arena[:], F32)
        dump("WBc", ZA[:].rearrange("p a b -> p (a b)"), BF16)
        dump("WCc", ZB[:].rearrange("p a b -> p (a b)"), BF16)
        dump("Aco", Aco[:].rearrange("p a b c -> p (a b c)"), F32)
        dump("Bco", Bco[:].rearrange("p a b c -> p (a b c)"), F32)
        dump("s_snew", GS.snew[:].rearrange("p a b c -> p (a b c)"), F32)
    S.emit(stack)
    stack.close()
    return nc


_NC_CACHE = {}


def _feat(v, n):
    return np.ascontiguousarray(np.asarray(v).reshape(n, 128).T)


def prep(inp):
    f32 = np.float32
    g = {k: np.asarray(v) for k, v in inp.items()}
    perm_heads = [h for m in range(8) for h in (m, 8 + m)]
    qperm = np.concatenate([np.arange(h * 64, (h + 1) * 64) for h in perm_heads])
    w_in = g["w_in"].astype(f32).copy()
    w_in[:, :, :1024] = g["w_in"][:, :, qperm]
    w_out = g["w_out"].astype(f32).copy()
    w_out[:, :1024, :] = g["w_out"][:, qperm, :]
    aog = g["attn_out_norm_g"][:, qperm]
    sinks_p = g["attn_sinks"][:, perm_heads]

    vec16 = np.zeros((128, (2 * L + 1) * 16), f32)
    for l in range(L):
        vec16[:, (l * 2) * 16:(l * 2 + 1) * 16] = _feat(g["attn_norm_g"][l], 16)
        vec16[:, (l * 2 + 1) * 16:(l * 2 + 2) * 16] = _feat(g["ffn_norm_g"][l], 16)
    vec16[:, (2 * L) * 16:(2 * L + 1) * 16] = _feat(g["final_norm_g"], 16)
    vec8 = np.zeros((128, L * 4 * 8), f32)
    for l in range(L):
        vec8[:, (l * 4 + 0) * 8:(l * 4 + 1) * 8] = _feat(aog[l], 8)
        vec8[:, (l * 4 + 1) * 8:(l * 4 + 2) * 8] = _feat(g["ssm_out_norm_g"][l], 8)
        vec8[:, (l * 4 + 2) * 8:(l * 4 + 3) * 8] = _feat(g["ssm_d"][l], 8)
        vec8[:, (l * 4 + 3) * 8:(l * 4 + 4) * 8] = _feat(g["b_glu"][l], 8)
    convp = np.zeros((L, 128, 4, 88), f32)
    for l in range(L):
        convp[l, :, 0:3, :] = g["conv_w"][l].reshape(3, 88, 128).transpose(2, 0, 1)
        convp[l, :, 3, :] = g["conv_b"][l].reshape(88, 128).T
    sinks = np.ascontiguousarray(np.broadcast_to(sinks_p.reshape(1, L * 16), (128, L * 16))).astype(f32)

    half = 32
    inv = (np.float32(10000.0) ** (-(np.arange(half, dtype=f32) / np.float32(half)))).astype(f32)
    pidx = np.arange(128)
    sgn = np.where((pidx % 64) >= 32, -1.0, 1.0).astype(f32)

    def rope_tabs(pos):
        ang = pos.astype(f32)[:, None] * inv[None, :]
        c = np.cos(ang).astype(f32)
        s_ = np.sin(ang).astype(f32)
        cosT = np.ascontiguousarray(c[:, pidx % 32].T)
        sinT = np.ascontiguousarray((s_[:, pidx % 32] * sgn[None, :]).T)
        return cosT.astype(f32), sinT.astype(f32)

    cosT, sinT = rope_tabs(np.arange(SEQ))
    coss, sins = rope_tabs(np.full((NS,), 16384))
    ident = np.eye(128, dtype=f32)
    partner = np.where((pidx % 64) < 32, pidx + 32, pidx - 32)
    pswap = np.zeros((128, 128), f32)
    pswap[partner, pidx] = 1.0
    qi = np.arange(128)[:, None]
    kj = np.arange(256)[None, :]
    valid = (kj >= qi) & (kj <= qi + 128)
    maskN = np.where(valid, 0.0, -60000.0).astype(f32)
    mask0 = np.where(valid & (kj >= 128), 0.0, -60000.0).astype(f32)

    rmask = np.zeros((128, 2), f32)
    rmask[:, 0] = ((pidx // 32) % 2 == 0)
    rmask[:, 1] = ((pidx // 32) % 2 == 1)
    bexp = np.zeros((L, 2, 128, 1024), f32)
    pwb = np.zeros((L, 3, 128, 1024), f32)
    cexp = np.zeros((L, 2, 128, 1024), f32)
    pwc = np.zeros((L, 3, 128, 32), f32)
    for l in range(L):
        for ri, key in enumerate(("ssm_b_re", "ssm_b_im")):
            Bt = g[key][l].reshape(8, 4, 2, 64, 16)
            out = np.zeros((4, 2, 16, 8, 2, 64), f32)
            for gl in range(2):
                out[:, gl, :, :, gl, :] = Bt[:, :, gl].transpose(1, 3, 0, 2)
            bexp[l, ri] = out.reshape(128, 1024)
        for ri, key in enumerate(("ssm_c_re", "ssm_c_im")):
            C = g[key][l].reshape(32, 2, 16, 64)
            out = np.zeros((2, 64, 32, 2, 16), f32)
            for gl in range(2):
                out[gl, :, :, gl, :] = C[:, gl].transpose(2, 0, 1)
            cexp[l, ri] = out.reshape(128, 1024)
        params = [g["ssm_a_re"][l], g["ssm_a_im"][l],
                  np.broadcast_to(g["ssm_log_dt"][l][:, None], (G, P))]
        for k, A in enumerate(params):
            A = np.asarray(A, f32)
            a4 = A.reshape(8, 4, 2, 64).transpose(1, 0, 2, 3)
            pwb[l, k] = np.broadcast_to(a4[:, None], (4, 32, 8, 2, 64)).reshape(128, 1024)
            pwc[l, k] = A.reshape(32, 2, 64).transpose(1, 2, 0).reshape(128, 32)

    shared = dict(w_in=w_in, w_glu=np.ascontiguousarray(g["w_glu"], f32), w_out=w_out,
                  w_up=np.ascontiguousarray(g["w_up"], f32), w_down=np.ascontiguousarray(g["w_down"], f32),
                  vec16=vec16, vec8=vec8, convp=convp, sinks=sinks, cosT=cosT, sinT=sinT, coss=coss, sins=sins,
                  ident=ident, pswap=pswap, maskN=maskN, mask0=mask0, rmask=rmask, bexp=bexp, pwb=pwb, cexp=cexp, pwc=pwc)
    in_maps = []
    for c in range(NCORES):
        sq = c % 4
        bs = slice(NS * c, NS * (c + 1))
        m = dict(shared)
        m["xT"] = np.ascontiguousarray(g["x_prompt"][sq].T, f32)
        m["xsT"] = np.ascontiguousarray(g["x_sample"][bs, 0, :].T, f32)
        ck = g["cache_k"][:, bs].reshape(L, NS, 128, 128)
        cv = g["cache_v"][:, bs].reshape(L, NS, 128, 128)
        m["ck"] = np.ascontiguousarray(ck, f32)
        m["ckT"] = np.ascontiguousarray(ck.transpose(0, 1, 3, 2), f32)
        m["cv"] = np.ascontiguousarray(cv, f32)
        h0 = np.zeros((L, 128, 2, 32, NS), f32)
        for ri, key in enumerate(("state_ssm_re", "state_ssm_im")):
            st = g[key][:, bs].reshape(L, NS, 32, 2, 64)
            h0[:, :, ri] = st.transpose(0, 3, 4, 2, 1).reshape(L, 128, 32, NS)
        m["h0"] = h0
        sc = g["state_conv"][:, bs].reshape(L, NS, 2, 88, 128)
        m["sconv"] = np.ascontiguousarray(sc.transpose(0, 4, 3, 2, 1), f32)
        in_maps.append(m)

    return in_maps


def kernel(**inp):
    f32 = np.float32
    in_maps = prep(inp)
    if "nc" not in _NC_CACHE:
        _NC_CACHE["nc"] = build_nc()
    nc = _NC_CACHE["nc"]
    res = run_bass_kernel_spmd(nc, in_maps, core_ids=list(range(NCORES)))
    R = res.results

    y_prompt = np.zeros((4, SEQ, D), f32)
    y_sample = np.zeros((32, 1, D), f32)
    k_prompt = np.zeros((L, 4, 128, 2, 64), f32)
    v_prompt = np.zeros((L, 4, 128, 2, 64), f32)
    sre_p = np.zeros((L, 4, G, P), f32)
    sim_p = np.zeros((L, 4, G, P), f32)
    conv_p = np.zeros((L, 4, 2, 2 * DFF), f32)
    k_sample = np.zeros((L, 32, 128, 2, 64), f32)
    v_sample = np.zeros((L, 32, 128, 2, 64), f32)
    sre_s = np.zeros((L, 32, G, P), f32)
    sim_s = np.zeros((L, 32, G, P), f32)
    conv_s = np.zeros((L, 32, 2, 2 * DFF), f32)
    for c in range(NCORES):
        r = R[c]
        bs = slice(NS * c, NS * (c + 1))
        y_sample[bs, 0, :] = np.asarray(r["ysT"]).T
        k_sample[:, bs] = np.asarray(r["ks"]).reshape(L, NS, 128, 2, 64)
        v_sample[:, bs] = np.asarray(r["vs"]).reshape(L, NS, 128, 2, 64)
        ss = np.asarray(r["ssms"]).reshape(L, 2, 64, 2, 32, NS)
        st = ss.transpose(0, 3, 5, 4, 1, 2).reshape(L, 2, NS, G, P)
        sre_s[:, bs] = st[:, 0]
        sim_s[:, bs] = st[:, 1]
        cs = np.asarray(r["convs"])
        conv_s[:, bs] = cs.transpose(0, 4, 3, 2, 1).reshape(L, NS, 2, 2 * DFF)
        if c < 4:
            y_prompt[c] = np.asarray(r["yT"]).T
            k_prompt[:, c] = np.asarray(r["kp"]).transpose(0, 2, 1).reshape(L, 128, 2, 64)
            v_prompt[:, c] = np.asarray(r["vp"]).transpose(0, 2, 1).reshape(L, 128, 2, 64)
            sp = np.asarray(r["ssmp"]).reshape(L, 2, 64, 2, 32)
            sp = sp.transpose(0, 3, 4, 1, 2).reshape(L, 2, G, P)
            sre_p[:, c] = sp[:, 0]
            sim_p[:, c] = sp[:, 1]
            cp = np.asarray(r["convpo"])
            conv_p[:, c] = cp.transpose(0, 3, 2, 1).reshape(L, 2, 2 * DFF)
    return (y_prompt, y_sample, k_prompt, v_prompt, sre_p, sim_p, conv_p,
            k_sample, v_sample, sre_s, sim_s, conv_s)
```

```python
import math
from contextlib import ExitStack
import numpy as np
import concourse.bass as bass
import concourse.mybir as mybir
from concourse.bass_utils import run_bass_kernel_spmd

F32 = mybir.dt.float32
BF16 = mybir.dt.bfloat16
I32 = mybir.dt.int32
ALU = mybir.AluOpType
AF = mybir.ActivationFunctionType
ESZ = {F32: 4, BF16: 2, I32: 4}

L = 4
D = 2048
NCH = 16
NT = 512
NTILES = 4
SEQ = 2048
NS = 4
DFF = 5632
INC = 2304
G = 64
P = 64
NCORES = 8


MAXMT = None
ASTOP = None
_acount = [0]


def astep():
    _acount[0] += 1
    if ASTOP is not None and _acount[0] >= ASTOP:
        raise _Stop()


NOSAMP = False
LIMIT = None
DBG = False


class _Stop(Exception):
    pass


class Sched:
    K = 8
    ENGMAP = {"pe": "tensor", "dve": "vector", "act": "scalar", "pool": "gpsimd", "sp": "sync"}

    def __init__(self, nc):
        self.nc = nc
        self.streams = {}
        self.lastw = {}
        self.readers = {}
        self.track = set()

    def _keys(self, ap):
        space = str(ap.space)
        apl = list(ap.ap)
        esz = ESZ[ap.dtype]
        name = ap.tensor.name
        if "SB" not in space and "PSUM" not in space:
            if name not in self.track:
                return []
            lo = ap.offset
            hi = lo
            for s, c in apl:
                hi += (c - 1) * abs(s)
            return [(name, g) for g in range(lo * esz // 65536, hi * esz // 65536 + 1)]
        ps = apl[0][0]
        lo = ap.offset % ps if ps > 0 else ap.offset
        hi = lo
        for s, c in apl[1:]:
            hi += (c - 1) * abs(s)
        gran = 2048 if "PSUM" in space else 256
        return [(name, g) for g in range(lo * esz // gran, hi * esz // gran + 1)]

    def add(self, stream, fn, reads, writes, dma=False):
        st = self.streams.setdefault(stream, [])
        seq = len(st)
        deps = {}

        def dep(w):
            if w is None:
                return
            s2, q2 = w
            if s2 == stream and stream == "pe":
                return
            if self.streams[s2][q2]["dma"]:
                deps[(s2, q2)] = q2
            elif deps.get(s2, -1) < q2:
                deps[s2] = q2

        rk = [k for ap in reads for k in self._keys(ap)]
        wk = [k for ap in writes for k in self._keys(ap)]
        for k in rk:
            dep(self.lastw.get(k))
        for k in wk:
            dep(self.lastw.get(k))
            for s2, q2 in self.readers.get(k, {}).items():
                dep((s2, q2))
        for k in rk:
            self.readers.setdefault(k, {})[stream] = seq
        for k in wk:
            self.lastw[k] = (stream, seq)
            self.readers[k] = {}
        st.append(dict(fn=fn, deps=deps, dma=dma, needed=False))

    def emit(self, stack):
        nc = self.nc
        K = self.K
        for st in self.streams.values():
            for op in st:
                for s2, q2 in op["deps"].items():
                    s2 = s2[0] if isinstance(s2, tuple) else s2
                    self.streams[s2][q2]["needed"] = True
        csem = {}
        dsem = {}
        for name, st in self.streams.items():
            c = 0
            d = 0
            for op in st:
                if op["dma"]:
                    op["didx"] = d
                    d += 1
                else:
                    if op["needed"]:
                        c += 1
                    op["cnt"] = c
            if c > 0:
                csem[name] = stack.enter_context(nc.semaphore("c_" + name))
            if d > 0:
                dsem[name] = [stack.enter_context(nc.semaphore("d%d_%s" % (i, name))) for i in range(min(K, d))]
        streams = self.streams

        def run(name, eng):
            seen = {}

            def wait(sem, key, val):
                if seen.get(key, 0) >= val:
                    return
                eng.wait_ge(sem, val)
                seen[key] = val

            for op in streams[name]:
                for s2, q2 in op["deps"].items():
                    s2 = s2[0] if isinstance(s2, tuple) else s2
                    o2 = streams[s2][q2]
                    if o2["dma"]:
                        n = o2["didx"]
                        wait(dsem[s2][n % K], (s2, n % K), 16 * (n // K + 1))
                    else:
                        wait(csem[s2], (s2, "c"), o2["cnt"])
                if op["dma"]:
                    n = op["didx"]
                    if n >= K:
                        wait(dsem[name][n % K], (name, n % K), 16 * (n // K))
                    ins = op["fn"](eng)
                    ins.then_inc(dsem[name][n % K], 16)
                else:
                    ins = op["fn"](eng)
                    if op["needed"]:
                        ins.then_inc(csem[name], 1)
            nd = sum(1 for op in streams[name] if op["dma"])
            if nd > 0:
                for i in range(min(K, nd)):
                    cnt = (nd - i + K - 1) // K
                    wait(dsem[name][i], (name, i), 16 * cnt)

        block = stack.enter_context(nc.Block())
        for name in streams:
            getattr(block, self.ENGMAP[name])(lambda eng, name=name: run(name, eng))


def isap(x):
    return not isinstance(x, (int, float)) and x is not None


def build_nc():
    nc = bass.Bass("TRN2", target_bir_lowering=False)
    S = Sched(nc)
    stack = ExitStack()

    def din(name, shape, dt=F32):
        return nc.dram_tensor(name, list(shape), dt, kind="ExternalInput").ap()

    def dout(name, shape, dt=F32):
        return nc.dram_tensor(name, list(shape), dt, kind="ExternalOutput").ap()

    def sb(name, shape, dt=F32):
        return stack.enter_context(nc.sbuf_tensor("sb_" + name, list(shape), dt))

    xT_d = din("xT", [D, SEQ])
    xsT_d = din("xsT", [D, NS])
    w_in_d = din("w_in", [L, D, INC])
    w_glu_d = din("w_glu", [L, 1024, 1024])
    w_out_d = din("w_out", [L, D, D])
    w_up_d = din("w_up", [L, D, 2 * DFF])
    w_down_d = din("w_down", [L, DFF, D])
    vec16_d = din("vec16", [128, (2 * L + 1) * 16])
    vec8_d = din("vec8", [128, L * 4 * 8])
    convp_d = din("convp", [L, 128, 4, 88])
    sinks_d = din("sinks", [128, L * 16])
    cosT_d = din("cosT", [128, SEQ])
    sinT_d = din("sinT", [128, SEQ])
    coss_d = din("coss", [128, NS])
    sins_d = din("sins", [128, NS])
    ident_d = din("ident", [128, 128])
    pswap_d = din("pswap", [128, 128])
    maskN_d = din("maskN", [128, 256])
    mask0_d = din("mask0", [128, 256])
    rmask_d = din("rmask", [128, 2])
    bexp_d = din("bexp", [L, 2, 128, 1024])
    pwb_d = din("pwb", [L, 3, 128, 1024])
    cexp_d = din("cexp", [L, 2, 128, 1024])
    pwc_d = din("pwc", [L, 3, 128, 32])
    ckT_d = din("ckT", [L, NS, 128, 128])
    ck_d = din("ck", [L, NS, 128, 128])
    cv_d = din("cv", [L, NS, 128, 128])
    h0_d = din("h0", [L, 128, 2, 32, NS])
    sconv_d = din("sconv", [L, 128, 88, 2, NS])

    yT_o = dout("yT", [D, SEQ])
    ysT_o = dout("ysT", [D, NS])
    kp_o = dout("kp", [L, 128, 128])
    vp_o = dout("vp", [L, 128, 128])
    ssmp_o = dout("ssmp", [L, 128, 2, 32])
    convp_o = dout("convpo", [L, 128, 88, 2])
    ks_o = dout("ks", [L, NS, 128, 128])
    vs_o = dout("vs", [L, NS, 128, 128])
    ssms_o = dout("ssms", [L, 128, 2, 32, NS])
    convs_o = dout("convs", [L, 128, 88, 2, NS])

    PS = stack.enter_context(nc.psum_tensor("ps", [128, 7, 512], F32))
    PSB = stack.enter_context(nc.psum_tensor("psb", [128, 1024], BF16))
    Wb = [sb("w%d" % i, [128, 8192], BF16) for i in range(2)]
    ident_f = sb("ident_f", [128, 128])
    ident_b = sb("ident_b", [128, 128], BF16)
    pswap_f = sb("pswap_f", [128, 128])
    pswap_b = sb("pswap_b", [128, 128], BF16)
    ones_b = sb("ones_b", [128, 128], BF16)
    maskN = sb("maskN", [128, 256])
    mask0 = sb("mask0", [128, 256])
    vec16 = sb("vec16", [128, (2 * L + 1) * 16])
    vec8 = sb("vec8", [128, L * 4 * 8])
    bhalf = sb("bhalf", [128, L * 8])
    sinks = sb("sinks", [128, L * 16])
    negsink = sb("negsink", [128, L * 16])
    eps_t = sb("eps_t", [128, 1])
    convp = sb("convp", [128, 4, 88])
    ZA = sb("ZA", [128, 2, 1024], BF16)
    ZB = sb("ZB", [128, 2, 1024], BF16)
    WCz = sb("WCz", [128, 2, 8, 4, 64], BF16)
    rmask = sb("rmask", [128, 2])
    wsc = nc.dram_tensor("wsc", [L, 128, 2, 2048], BF16, kind="Internal").ap()
    S.track.add("wsc")
    Aco = sb("Aco", [128, L, 2, 32])
    Bco = sb("Bco", [128, L, 2, 32])
    kprev = sb("kprev", [128, L, 128], BF16)
    vprev = sb("vprev", [128, L, 128], BF16)
    Scar = sb("Scar", [128, L, 2, 32])
    tails = sb("tails", [128, L, 88, 2])
    sqs = [sb("sq%d" % i, [128, NT], BF16) for i in range(2)]
    stdt = sb("stdt", [128, NT])
    rstd = sb("rstd", [128, NT])
    ropeA = sb("ropeA", [128, NT], BF16)
    ropeB = sb("ropeB", [128, NT], BF16)
    st_rmax = sb("st_rmax", [128, 16])
    st_nb = sb("st_nb", [128, 16])
    st_rsum = sb("st_rsum", [128, 16])
    st_es = sb("st_es", [128, 16])
    st_den = sb("st_den", [128, 16])
    st_rden = sb("st_rden", [128, 16])
    arena = sb("arena", [128, 10304])
    ShL = arena[:, 0:4096].rearrange("p (r g k) -> p r g k", r=2, g=32)
    tmp3 = arena[:, 4096:4352].rearrange("p (g s) -> p g s", s=8)
    lt1 = arena[:, 6144:6656]
    lt2 = arena[:, 6656:7168]
    Ccar = arena[:, 7168:7744].rearrange("p (c r g) -> p c r g", r=2, g=32)
    Ptab = arena[:, 7744:8256].rearrange("p (r g s) -> p r g s", r=2, g=32)
    Shb = arena[:, 8256:10304].bitcast(BF16).rearrange("p (r g k) -> p r g k", r=2, g=32)
    actT = arena[:, 0:5632].bitcast(BF16).rearrange("p (i n) -> p i n", n=NT)
    setA = arena[:, 0:6144].rearrange("p (a w) -> p a w", w=512)
    WBc = arena[:, 8256:9280].bitcast(BF16).rearrange("p (r c) -> p r c", r=2)
    WCc = arena[:, 9280:10304].bitcast(BF16).rearrange("p (r c) -> p r c", r=2)
    Sm = [arena[:, i * 256:(i + 1) * 256] for i in range(2)]
    Pb = [arena[:, 512 + i * 128:512 + (i + 1) * 128].bitcast(BF16) for i in range(2)]
    PTs = [arena[:, 768 + i * 128:768 + (i + 1) * 128].bitcast(BF16) for i in range(2)]
    Otok = arena[:, 1024:1536].bitcast(BF16)
    stg = arena[:, 6144:7168].bitcast(BF16).rearrange("p (a w) -> p a w", w=512)
    extg = arena[:, 5632:5632 + NT + 2]
    extv = arena[:, 6152:6152 + NT + 2]
    cg = arena[:, 6672:6672 + NT]
    cvv = arena[:, 7184:7184 + NT]
    sc_t1 = sb("sc_t1", [128, 2, 32])
    sc_t2 = sb("sc_t2", [128, 2, 32])
    A8 = sb("A8", [128, 2, 32])
    B8 = sb("B8", [128, 2, 32])
    tmpa = sb("tmpa", [128, NT])
    tmpb = sb("tmpb", [128, NT])

    class Grp:
        pass

    def mkgrp(name, N, nunits):
        g = Grp()
        g.name = name
        g.N = N
        g.xT = sb(name + "_xT", [128, NCH, N])
        g.hT = sb(name + "_hT", [128, NCH, N], BF16)
        g.qT = sb(name + "_qT", [128, 8, N], BF16)
        g.uT = sb(name + "_uT", [128, 8, N], BF16)
        g.yT = g.qT
        g.krot = sb(name + "_krot", [128, N])
        g.vTf = sb(name + "_vTf", [128, N])
        g.vTb = sb(name + "_vTb", [128, N], BF16)
        g.cos = sb(name + "_cos", [128, N])
        g.sin = sb(name + "_sin", [128, N])
        return g

    GP = mkgrp("p", NT, 1)
    GS = mkgrp("s", NS, NS)
    GP.kT = sb("p_kT", [128, 128 + NT], BF16)
    GP.Vt = sb("p_Vt", [128, 5, 128], BF16)
    GS.kT = sb("s_kT", [128, NS, 256], BF16)
    GS.Vt = sb("s_Vt", [128, NS, 2, 128], BF16)
    GS.h0 = sb("s_h0", [128, 2, 32, NS])
    GS.sconv = sb("s_sconv", [128, 88, 2, NS])
    GS.snew = sb("s_snew", [128, 2, 32, NS])
    GS.snb = sb("s_snb", [128, 2, 32, NS], BF16)
    GS.cout = sb("s_cout", [128, 88, 2, NS])
    GS.Xs = sb("s_Xs", [128, 2, 32, NS])
    GS.yT_act = sb("s_yTact", [128, 22, NS], BF16)

    def MM(out, lhsT, rhs, start=True, stop=True):
        S.add("pe", lambda e: e.matmul(out, lhsT=lhsT, rhs=rhs, start=start, stop=stop), [lhsT, rhs], [out])

    def TR(out, in_, ident):
        S.add("pe", lambda e: e.transpose(out, in_, ident), [in_, ident], [out])

    def ACT(out, in_, func, bias=None, scale=None, accum=None):
        kw = {}
        rd = [in_]
        wr = [out]
        if bias is not None:
            kw["bias"] = bias
            if isap(bias):
                rd.append(bias)
        if scale is not None:
            kw["scale"] = scale
            if isap(scale):
                rd.append(scale)
        if accum is not None:
            kw["accum_out"] = accum
            wr.append(accum)
        S.add("act", lambda e: e.activation(out=out, in_=in_, func=func, **kw), rd, wr)

    def TT(out, a, b, op, eng="dve"):
        S.add(eng, lambda e: e.tensor_tensor(out=out, in0=a, in1=b, op=op), [a, b], [out])

    def TS(out, a, s1, s2, op0, op1=None, eng="dve"):
        rd = [a] + [x for x in (s1, s2) if isap(x)]
        if op1 is None:
            S.add(eng, lambda e: e.tensor_scalar(out=out, in0=a, scalar1=s1, scalar2=None, op0=op0), rd, [out])
        else:
            S.add(eng, lambda e: e.tensor_scalar(out=out, in0=a, scalar1=s1, scalar2=s2, op0=op0, op1=op1), rd, [out])

    def STT(out, a, sc, b, op0, op1):
        rd = [a, b] + ([sc] if isap(sc) else [])
        S.add("dve", lambda e: e.scalar_tensor_tensor(out=out, in0=a, scalar=sc, in1=b, op0=op0, op1=op1), rd, [out])

    def CP(out, in_, eng="dve"):
        if eng == "act":
            ACT(out, in_, AF.Copy)
        else:
            S.add(eng, lambda e: e.tensor_copy(out=out, in_=in_), [in_], [out])

    def RECIP(out, in_):
        S.add("dve", lambda e: e.reciprocal(out=out, in_=in_), [in_], [out])

    def MEMSET(ap, val):
        S.add("dve", lambda e: e.memset(ap, val), [], [ap])

    def DMA(out, in_, q="sp"):
        S.add(q, lambda e: e.dma_start(out=out, in_=in_), [in_], [out], dma=True)

    def REDMAX(out, in_):
        S.add("dve", lambda e: e.tensor_reduce(out=out, in_=in_, axis=mybir.AxisListType.X, op=ALU.max), [in_], [out])

    def TTR(out, a, b, op0, op1, init, accum):
        S.add("dve", lambda e: e.tensor_tensor_reduce(out=out, in0=a, in1=b, scale=1.0, scalar=init, op0=op0, op1=op1,
                                                     accum_out=accum), [a, b], [out, accum])

    DMA(ident_f[:], ident_d)
    DMA(pswap_f[:], pswap_d)
    DMA(maskN[:], maskN_d)
    DMA(mask0[:], mask0_d)
    DMA(vec16[:], vec16_d)
    DMA(vec8[:], vec8_d)
    DMA(sinks[:], sinks_d)
    DMA(rmask[:], rmask_d)
    CP(ident_b[:], ident_f[:])
    CP(pswap_b[:], pswap_f[:])
    MEMSET(ones_b[:], 1.0)
    MEMSET(eps_t[:], 1e-6)
    TS(negsink[:], sinks[:], -1.0, None, ALU.mult)
    MEMSET(kprev[:], 0.0)
    MEMSET(vprev[:], 0.0)
    MEMSET(Scar[:], 0.0)
    MEMSET(tails[:], 0.0)
    MEMSET(GS.kT[:], 0.0)
    MEMSET(GS.Vt[:], 0.0)
    MEMSET(WCz[:], 0.0)
    for l in range(L):
        TS(bhalf[:, l * 8:(l + 1) * 8], vec8[:, (l * 4 + 3) * 8:(l * 4 + 4) * 8], 0.5, None, ALU.mult)

    def v16(l, which):
        o = (l * 2 + which) * 16
        return vec16[:, o:o + 16]

    def v8(l, which):
        o = (l * 4 + which) * 8
        return vec8[:, o:o + 8]

    TWO_PI = 2.0 * math.pi

    def sincos(dst, theta, shift, W, tmp1, tmp2i, tmp3):
        TS(tmp1, theta, 1.0 / TWO_PI, shift, ALU.mult, ALU.add)
        CP(tmp2i, tmp1)
        CP(tmp3, tmp2i)
        TT(tmp1, tmp1, tmp3, ALU.subtract)
        TS(tmp3, tmp1, 0.5, None, ALU.is_gt)
        TT(tmp1, tmp1, tmp3, ALU.subtract)
        TS(tmp3, tmp1, -0.5, None, ALU.is_lt)
        TT(tmp1, tmp1, tmp3, ALU.add)
        ACT(dst, tmp1, AF.Sin, scale=6.283185)

    def lam_q(W, ar, ai, ldt, lr, li, qr, qi, t1, t2i, t3, t4):
        ACT(ldt, ldt, AF.Exp)
        TT(t4, ai, ldt, ALU.mult)
        sincos(li, t4, 0.0, W, t1, t2i, t3)
        sincos(lr, t4, 0.25, W, t1, t2i, t3)
        TT(t4, ar, ldt, ALU.mult)
        ACT(t4, t4, AF.Exp)
        TT(lr, lr, t4, ALU.mult)
        TT(li, li, t4, ALU.mult)
        TT(t1, ar, ar, ALU.mult)
        TT(t3, ai, ai, ALU.mult)
        TT(t1, t1, t3, ALU.add)
        RECIP(t1, t1)
        TS(t4, lr, -1.0, None, ALU.add)
        TT(qr, t4, ar, ALU.mult)
        TT(t3, li, ai, ALU.mult)
        TT(qr, qr, t3, ALU.add)
        TT(qr, qr, t1, ALU.mult)
        TT(qi, li, ar, ALU.mult)
        TT(t3, t4, ai, ALU.mult)
        TT(qi, qi, t3, ALU.subtract)
        TT(qi, qi, t1, ALU.mult)

    for l in range(L):
        for hv in range(2):
            hs = slice(hv * 512, (hv + 1) * 512)
            ar, ai, ldt = setA[:, 0, :], setA[:, 1, :], setA[:, 2, :]
            lr, li, qr, qi = setA[:, 3, :], setA[:, 4, :], setA[:, 5, :], setA[:, 6, :]
            t1, t3, t4 = setA[:, 7, :], setA[:, 8, :], setA[:, 9, :]
            t2i = setA[:, 10, :].bitcast(I32)
            bre, bim = setA[:, 10, :], setA[:, 11, :]
            DMA(ar, pwb_d[l, 0, :, hs])
            DMA(ai, pwb_d[l, 1, :, hs])
            DMA(ldt, pwb_d[l, 2, :, hs])
            lam_q(512, ar, ai, ldt, lr, li, qr, qi, t1, t2i, t3, t4)
            DMA(bre, bexp_d[l, 0, :, hs])
            DMA(bim, bexp_d[l, 1, :, hs])
            TT(t1, qr, bre, ALU.mult)
            TT(t3, qi, bim, ALU.mult)
            TT(stg[:, 0, :], t1, t3, ALU.subtract)
            TT(t1, qr, bim, ALU.mult)
            TT(t3, qi, bre, ALU.mult)
            TT(stg[:, 1, :], t1, t3, ALU.add)
            cre, cim = setA[:, 10, :], setA[:, 11, :]
            DMA(cre, cexp_d[l, 0, :, hs])
            DMA(cim, cexp_d[l, 1, :, hs])
            CP(stg[:, 2, :], cre)
            TS(stg[:, 3, :], cim, -1.0, None, ALU.mult)
            DMA(wsc[l, :, 0, hv * 512:(hv + 1) * 512], stg[:, 0, :])
            DMA(wsc[l, :, 0, 1024 + hv * 512:1024 + (hv + 1) * 512], stg[:, 1, :])
            DMA(wsc[l, :, 1, hv * 512:(hv + 1) * 512], stg[:, 2, :])
            DMA(wsc[l, :, 1, 1024 + hv * 512:1024 + (hv + 1) * 512], stg[:, 3, :])
        sar, sai, sdt = setA[:, 0, 0:32], setA[:, 1, 0:32], setA[:, 2, 0:32]
        slr, sli, sqr, sqi = setA[:, 3, 0:32], setA[:, 4, 0:32], setA[:, 5, 0:32], setA[:, 6, 0:32]
        s1, s3, s4 = setA[:, 7, 0:32], setA[:, 8, 0:32], setA[:, 9, 0:32]
        s2i = setA[:, 10, 0:32].bitcast(I32)
        DMA(sar, pwc_d[l, 0])
        DMA(sai, pwc_d[l, 1])
        DMA(sdt, pwc_d[l, 2])
        lam_q(32, sar, sai, sdt, slr, sli, sqr, sqi, s1, s2i, s3, s4)
        CP(Aco[:, l, 0, :], slr)
        CP(Aco[:, l, 1, :], slr)
        TS(Bco[:, l, 0, :], sli, -1.0, None, ALU.mult)
        CP(Bco[:, l, 1, :], sli)

    state = dict(wi=0, bank=0)

    def wtile(view, KC, c0, cw, parts=None):
        buf = Wb[state["wi"] % 2]
        state["wi"] += 1
        t = buf[:, 0:KC * cw].rearrange("p (k c) -> p k c", c=cw)
        if parts is None:
            DMA(t, view[:, :, c0:c0 + cw], q="pool")
        else:
            o = 0
            for (v2, a0, aw) in parts:
                DMA(t[:, :, o:o + aw], v2[:, :, a0:a0 + aw], q="pool")
                o += aw
        return t

    def nextbank():
        b = state["bank"]
        state["bank"] = (b + 1) % 4
        return b

    def linear(groups, view, KC, ncols, rhs_of, consume, cwmax=512):
        c0 = 0
        while c0 < ncols:
            cw = min(cwmax, ncols - c0)
            t = wtile(view, KC, c0, cw)
            for m in range(cw // 128):
                mt = c0 // 128 + m
                if MAXMT is not None and mt >= MAXMT:
                    raise _Stop()
                for g in groups:
                    ps = PS[:, nextbank(), 0:g.N]
                    for kc in range(KC):
                        MM(ps, t[:, kc, m * 128:(m + 1) * 128], rhs_of(g, kc), start=(kc == 0), stop=(kc == KC - 1))
                    consume(g, mt, ps)
            c0 += cw

    def rmsnorm(g, src, nch, gain, dst, Dn):
        N = g.N
        pst = PS[:, 4, 0:N]
        for c in range(nch):
            sq = sqs[c % 2][:, 0:N]
            ACT(sq, src[:, c, :], AF.Square)
            MM(pst, ones_b[:], sq, start=(c == 0), stop=(c == nch - 1))
        ACT(stdt[:, 0:N], pst, AF.Sqrt, bias=eps_t[:, 0:1], scale=1.0 / Dn)
        RECIP(rstd[:, 0:N], stdt[:, 0:N])
        for c in range(nch):
            STT(dst[:, c, :], src[:, c, :], gain[:, c:c + 1], rstd[:, 0:N], ALU.mult, ALU.mult)

    def rope(g, ps, dst_bf, dst_f32=None):
        N = g.N
        TT(ropeA[:, 0:N], ps, g.cos[:, 0:N], ALU.mult)
        TT(ropeB[:, 0:N], ps, g.sin[:, 0:N], ALU.mult)
        ps2 = PS[:, 5, 0:N]
        MM(ps2, ident_b[:], ropeA[:, 0:N], start=True, stop=False)
        MM(ps2, pswap_b[:], ropeB[:, 0:N], start=False, stop=True)
        ACT(dst_bf, ps2, AF.Copy)
        if dst_f32 is not None:
            import os
            kv = os.environ.get("KVAR", "a")
            if kv == "a":
                ACT(dst_f32, ps2, AF.Copy)
            elif kv == "b":
                CP(dst_f32, ps2)
            else:
                pass

    def attn_unit(l, nq, qap_of, kT2, Vblk, mask, acols, g):
        for m in range(8):
            for e in range(2):
                hi = m * 2 + e
                col = l * 16 + hi
                Sps = PS[0:nq, 6 if e == 0 else 4, 0:256]
                MM(Sps, qap_of(m, e), kT2[e * 64:(e + 1) * 64, :])
                astep()
                TT(Sm[e][0:nq, :], Sps, mask[0:nq, :], ALU.add)
                astep()
                REDMAX(st_rmax[0:nq, hi:hi + 1], Sm[e][0:nq, :])
                astep()
                TS(st_nb[0:nq, hi:hi + 1], st_rmax[0:nq, hi:hi + 1], -0.125, negsink[0:nq, col:col + 1], ALU.mult, ALU.min)
                astep()
                ACT(Pb[e][0:nq, :], Sm[e][0:nq, :], AF.Exp, bias=st_nb[0:nq, hi:hi + 1], scale=0.125,
                    accum=st_rsum[0:nq, hi:hi + 1])
                astep()
                ACT(st_es[0:nq, hi:hi + 1], sinks[0:nq, col:col + 1], AF.Exp, bias=st_nb[0:nq, hi:hi + 1], scale=1.0)
                astep()
                TT(st_den[0:nq, hi:hi + 1], st_rsum[0:nq, hi:hi + 1], st_es[0:nq, hi:hi + 1], ALU.add)
                astep()
                RECIP(st_rden[0:nq, hi:hi + 1], st_den[0:nq, hi:hi + 1])
                astep()
                for kb in range(2):
                    TR(PSB[:, e * 256 + kb * 128:e * 256 + kb * 128 + nq], Pb[e][0:nq, kb * 128:(kb + 1) * 128],
                       ident_b[0:nq, 0:nq])
                    astep()
                for kb in range(2):
                    CP(PTs[e][:, kb * 128:kb * 128 + nq], PSB[:, e * 256 + kb * 128:e * 256 + kb * 128 + nq])
                    astep()
                Ops = PS[0:nq, 5, (hi % 8) * 64:(hi % 8 + 1) * 64]
                for kb in range(2):
                    MM(Ops, PTs[e][:, kb * 128:kb * 128 + nq], Vblk[kb][:, e * 64:(e + 1) * 64], start=(kb == 0), stop=(kb == 1))
                    astep()
                ACT(Otok[0:nq, hi * 64:(hi + 1) * 64], Ops, AF.Copy, scale=st_rden[0:nq, hi:hi + 1])
                astep()
        for m in range(8):
            o = 512 + (m % 4) * 128
            TR(PSB[:, o:o + nq], Otok[0:nq, m * 128:(m + 1) * 128], ident_b[0:nq, 0:nq])
            astep()
            CP(g.hT[:, m, acols], PSB[:, o:o + nq])
            astep()

    def gelu_inplace(g):
        N = g.N
        for c in range(8):
            y = g.yT[:, c, :]
            TT(tmpa[:, 0:N], y, y, ALU.mult)
            TS(tmpa[:, 0:N], tmpa[:, 0:N], 0.044715, 1.0, ALU.mult, ALU.add)
            TT(tmpa[:, 0:N], tmpa[:, 0:N], y, ALU.mult)
            ACT(tmpb[:, 0:N], tmpa[:, 0:N], AF.Tanh, scale=0.7978845608028654)
            TS(tmpb[:, 0:N], tmpb[:, 0:N], 0.5, 0.5, ALU.mult, ALU.add)
            TT(y, tmpb[:, 0:N], y, ALU.mult)

    xT_v = xT_d.rearrange("(c p) t -> p c t", p=128)
    yT_v = yT_o.rearrange("(c p) t -> p c t", p=128)
    xsT_v = xsT_d.rearrange("(c p) t -> p c t", p=128)
    ysT_v = ysT_o.rearrange("(c p) t -> p c t", p=128)

    def chk(j, l, p):
        if LIMIT is not None and (j, l, p) >= tuple(LIMIT):
            raise _Stop()

    def main_loop():
      for j in range(NTILES):
          t0 = j * NT
          groups = [GP] + ([GS] if (j == 0 and not NOSAMP) else [])
          chk(j, -1, 0)
          for c4 in range(4):
              DMA(GP.xT[:, c4 * 4:(c4 + 1) * 4, :], xT_v[:, c4 * 4:(c4 + 1) * 4, t0:t0 + NT])
          DMA(GP.cos[:], cosT_d[:, t0:t0 + NT])
          DMA(GP.sin[:], sinT_d[:, t0:t0 + NT])
          if j == 0 and not NOSAMP:
              DMA(GS.xT[:], xsT_v)
              DMA(GS.cos[:], coss_d)
              DMA(GS.sin[:], sins_d)
          for l in range(L):
              last = (j == NTILES - 1)
              chk(j, l, 0)
              for g in groups:
                  rmsnorm(g, g.xT, NCH, v16(l, 0), g.hT, float(D))
              chk(j, l, 0.1)
              CP(GP.kT[:, 0:128], kprev[:, l, :])
              CP(GP.Vt[:, 0, :], vprev[:, l, :])
              if j == 0 and not NOSAMP:
                  for b in range(NS):
                      DMA(GS.kT[:, b, 0:128], ckT_d[l, b], q="pool")
                      DMA(GS.Vt[:, b, 0, :], cv_d[l, b], q="pool")
                  chk(j, l, 0.12)
                  DMA(GS.h0[:], h0_d[l])
                  DMA(GS.sconv[:], sconv_d[l])
              DMA(convp[:], convp_d[l])
              chk(j, l, 0.13)
              DMA(WBc.rearrange("p r c -> p (r c)"), wsc[l, :, 0, :])
              DMA(WCc.rearrange("p r c -> p (r c)"), wsc[l, :, 1, :])
              for r_ in range(2):
                  TS(ZA[:, r_, :], WBc[:, r_, :], rmask[:, 0:1], None, ALU.mult)
                  TS(ZB[:, r_, :], WBc[:, r_, :], rmask[:, 1:2], None, ALU.mult)
              WCv = WCc.rearrange("p r (f q c) -> p r f q c", f=8, q=4)
              for r_ in range(2):
                  for q_ in range(4):
                      CP(WCz[:, r_, :, q_, (q_ % 2) * 32:(q_ % 2) * 32 + 32], WCv[:, r_, :, q_, :])

              lrT, liT = Aco[:, l, 0, :], Bco[:, l, 1, :]
              CP(Ptab[:, 0, :, 0], lrT)
              CP(Ptab[:, 1, :, 0], liT)
              for s_ in range(1, 8):
                  pr, pi_ = Ptab[:, 0, :, s_ - 1], Ptab[:, 1, :, s_ - 1]
                  TT(sc_t1[:, 0, :], pr, lrT, ALU.mult)
                  TT(sc_t1[:, 1, :], pi_, liT, ALU.mult)
                  TT(Ptab[:, 0, :, s_], sc_t1[:, 0, :], sc_t1[:, 1, :], ALU.subtract)
                  TT(sc_t2[:, 0, :], pr, liT, ALU.mult)
                  TT(sc_t2[:, 1, :], pi_, lrT, ALU.mult)
                  TT(Ptab[:, 1, :, s_], sc_t2[:, 0, :], sc_t2[:, 1, :], ALU.add)
              CP(A8[:, 0, :], Ptab[:, 0, :, 7])
              CP(A8[:, 1, :], Ptab[:, 0, :, 7])
              TS(B8[:, 0, :], Ptab[:, 1, :, 7], -1.0, None, ALU.mult)
              CP(B8[:, 1, :], Ptab[:, 1, :, 7])
              chk(j, l, 0.2)
              def cons_in(g, mt, ps):
                  N = g.N
                  if mt < 8:
                      rope(g, ps, g.qT[:, mt, :])
                  elif mt == 8:
                      if g is GP:
                          rope(g, ps, g.kT[:, 128:128 + N], g.krot[:, 0:N])
                      else:
                          rope(g, ps, g.kT[:, :, 128:129].rearrange("p b o -> p (b o)"), g.krot[:, 0:N])
                  elif mt == 9:
                      ACT(g.vTb[:, 0:N], ps, AF.Copy)
                      ACT(g.vTf[:, 0:N], ps, AF.Copy)
                  else:
                      ACT(g.uT[:, mt - 10, :], ps, AF.Copy)

              w_in_v = w_in_d[l].rearrange("(k p) c -> p k c", p=128)
              linear(groups, w_in_v, NCH, INC, lambda g, kc: g.hT[:, kc, :], cons_in)

              chk(j, l, 0.3)
              for blk in range(4):
                  o = (blk % 4) * 128
                  TR(PSB[:, 512 + o:512 + o + 128], GP.vTb[:, blk * 128:(blk + 1) * 128], ident_b[:])
                  CP(GP.Vt[:, blk + 1, :], PSB[:, 512 + o:512 + o + 128])
              chk(j, l, 0.4)
              CP(kprev[:, l, :], GP.kT[:, NT:NT + 128])
              CP(vprev[:, l, :], GP.Vt[:, 4, :])
              if last:
                  DMA(kp_o[l], GP.krot[:, NT - 128:NT])
                  DMA(vp_o[l], GP.vTf[:, NT - 128:NT])
              if j == 0 and not NOSAMP:
                  for b in range(NS):
                      MM(PS[0:1, 4, 0:128], GS.vTb[:, b:b + 1], ident_b[:])
                      CP(GS.Vt[0:1, b, 1, :], PS[0:1, 4, 0:128])
                      DMA(ks_o[l, b, 0:127, :], ck_d[l, b, 1:128, :])
                      DMA(vs_o[l, b, 0:127, :], cv_d[l, b, 1:128, :])
                      DMA(ks_o[l, b, 127, :].rearrange("(p o) -> p o", o=1), GS.krot[:, b:b + 1])
                      DMA(vs_o[l, b, 127, :].rearrange("(p o) -> p o", o=1), GS.vTf[:, b:b + 1])

              chk(j, l, 1)
              for qb in range(4):
                  mask = mask0 if (j == 0 and qb == 0) else maskN
                  attn_unit(l, 128,
                            lambda m, e, qb=qb: GP.qT[e * 64:(e + 1) * 64, m, qb * 128:(qb + 1) * 128],
                            GP.kT[:, qb * 128:qb * 128 + 256],
                            [GP.Vt[:, qb, :], GP.Vt[:, qb + 1, :]],
                            mask, slice(qb * 128, (qb + 1) * 128), GP)
              if j == 0 and not NOSAMP:
                  for b in range(NS):
                      attn_unit(l, 1,
                                lambda m, e, b=b: GS.qT[e * 64:(e + 1) * 64, m, b:b + 1],
                                GS.kT[:, b, :],
                                [GS.Vt[:, b, 0, :], GS.Vt[:, b, 1, :]],
                                maskN, slice(b, b + 1), GS)

              chk(j, l, 2)
              CP(Ccar[:, 0, :, :], Scar[:, l, :, :])
              for stt in range(8):
                  cs = slice(stt * 64, (stt + 1) * 64)
                  Xv = ShL.rearrange("p r (t q) k -> p r t (q k)", q=4)
                  for ri in range(2):
                      for bb in range(2):
                          for hh in range(2):
                              bank = 2 * hh + bb
                              for i8 in range(8):
                                  t = bb * 4 + i8 // 2
                                  q4 = 2 * hh + i8 % 2
                                  Z = ZA if q4 % 2 == 0 else ZB
                                  MM(PS[:, bank, i8 * 64:(i8 + 1) * 64],
                                     Z[hh * 64:(hh + 1) * 64, ri, t * 128:(t + 1) * 128],
                                     GP.uT[hh * 64:(hh + 1) * 64, t, cs])
                          for hh in range(2):
                              bank = 2 * hh + bb
                              ACT(Xv[:, ri, bb * 4:(bb + 1) * 4, hh * 128:(hh + 1) * 128],
                                  PS[:, bank, :].rearrange("p (a b) -> p a b", b=128), AF.Copy)
                  if stt == 0:
                      chk(j, l, 2.1)
                  if stt > 0:
                      CP(Ccar[:, 0, :, :], Ccar[:, 8, :, :])
                  Lv = ShL.rearrange("p r g (c s) -> p (r g) c s", s=8)
                  Lr = [ShL[:, r_, :, :].rearrange("p g (c s) -> p g c s", s=8) for r_ in range(2)]
                  Aflat = Aco[:, l, :, :].rearrange("p r g -> p (r g)").unsqueeze(2).broadcast_to([128, 64, 8])
                  Bh = [Bco[:, l, r_, :].unsqueeze(2).broadcast_to([128, 32, 8]) for r_ in range(2)]
                  t1v = lt1.rearrange("p (a c) -> p a c", c=8)
                  t2v = lt2.rearrange("p (r g c) -> p r g c", r=2, c=8)
                  for s_ in range(1, 8):
                      TT(t1v, Aflat, Lv[:, :, :, s_ - 1], ALU.mult)
                      TT(t2v[:, 0, :, :], Bh[0], Lr[1][:, :, :, s_ - 1], ALU.mult)
                      TT(t2v[:, 1, :, :], Bh[1], Lr[0][:, :, :, s_ - 1], ALU.mult)
                      TT(lt1, lt1, lt2, ALU.add)
                      TT(Lv[:, :, :, s_], Lv[:, :, :, s_], t1v, ALU.add)
                  for c_ in range(8):
                      TT(sc_t1[:], A8[:], Ccar[:, c_, :, :], ALU.mult)
                      TT(sc_t2[:, 0, :], B8[:, 0, :], Ccar[:, c_, 1, :], ALU.mult)
                      TT(sc_t2[:, 1, :], B8[:, 1, :], Ccar[:, c_, 0, :], ALU.mult)
                      TT(sc_t1[:], sc_t1[:], sc_t2[:], ALU.add)
                      TT(Ccar[:, c_ + 1, :, :].rearrange("p r g -> p (r g)"), sc_t1[:].rearrange("p r g -> p (r g)"),
                         Lv[:, :, c_, 7], ALU.add)
                  for c_ in range(8):
                      cre = Ccar[:, c_, 0, :].unsqueeze(2).broadcast_to([128, 32, 8])
                      cim = Ccar[:, c_, 1, :].unsqueeze(2).broadcast_to([128, 32, 8])
                      Lre, Lim = Lr[0][:, :, c_, :], Lr[1][:, :, c_, :]
                      TT(tmp3, Ptab[:, 0, :, :], cre, ALU.mult)
                      TT(Lre, Lre, tmp3, ALU.add)
                      TT(tmp3, Ptab[:, 1, :, :], cim, ALU.mult)
                      TT(Lre, Lre, tmp3, ALU.subtract)
                      TT(tmp3, Ptab[:, 0, :, :], cim, ALU.mult)
                      TT(Lim, Lim, tmp3, ALU.add)
                      TT(tmp3, Ptab[:, 1, :, :], cre, ALU.mult)
                      TT(Lim, Lim, tmp3, ALU.add)
                  if stt == 0:
                      chk(j, l, 2.2)
                  for r_ in range(2):
                      CP(Shb[:, r_, :, :], ShL[:, r_, :, :])
                  if stt == 0:
                      chk(j, l, 2.3)
                  for ft in range(8):
                      if ft % 8 == 0:
                          bank = nextbank()
                      yps = PS[:, bank, (ft % 8) * 64:(ft % 8 + 1) * 64]
                      for q4 in range(4):
                          gp = ft * 4 + q4
                          for ri in range(2):
                              hh = q4 // 2
                              MM(yps[hh * 64:(hh + 1) * 64, :], WCz[:, ri, ft, q4, :], Shb[:, ri, gp, :],
                                 start=(q4 % 2 == 0 and ri == 0), stop=(q4 % 2 == 1 and ri == 1))
                  for ft in range(8):
                      yps = PS[:, bank, (ft % 8) * 64:(ft % 8 + 1) * 64]
                      STT(GP.yT[:, ft, cs], GP.uT[:, ft, cs], v8(l, 2)[:, ft:ft + 1], yps, ALU.mult, ALU.add)
                  if stt == 0:
                      chk(j, l, 2.4)
              CP(Scar[:, l, :, :], Ccar[:, 8, :, :])
              chk(j, l, 2.5)
              if last:
                  DMA(ssmp_o[l], Ccar[:, 8, :, :])
              if j == 0 and not NOSAMP:
                  g = GS
                  Xvs = g.Xs[:].rearrange("p r (t q) k -> p r t (q k)", q=4)
                  for ri in range(2):
                      for bb in range(2):
                          for hh in range(2):
                              bank = 2 * hh + bb
                              for i8 in range(8):
                                  t = bb * 4 + i8 // 2
                                  q4 = 2 * hh + i8 % 2
                                  Z = ZA if q4 % 2 == 0 else ZB
                                  MM(PS[:, bank, i8 * NS:(i8 + 1) * NS],
                                     Z[hh * 64:(hh + 1) * 64, ri, t * 128:(t + 1) * 128],
                                     g.uT[hh * 64:(hh + 1) * 64, t, :])
                          for hh in range(2):
                              bank = 2 * hh + bb
                              CP(Xvs[:, ri, bb * 4:(bb + 1) * 4, hh * 2 * NS:(hh + 1) * 2 * NS],
                                 PS[:, bank, 0:8 * NS].rearrange("p (a b) -> p a b", b=2 * NS))
                  for b in range(NS):
                      TT(sc_t1[:], Aco[:, l, :, :], g.h0[:, :, :, b], ALU.mult)
                      TT(sc_t2[:, 0, :], Bco[:, l, 0, :], g.h0[:, 1, :, b], ALU.mult)
                      TT(sc_t2[:, 1, :], Bco[:, l, 1, :], g.h0[:, 0, :, b], ALU.mult)
                      TT(sc_t1[:], sc_t1[:], sc_t2[:], ALU.add)
                      TT(g.snew[:, :, :, b], sc_t1[:], g.Xs[:, :, :, b], ALU.add)
                  CP(g.snb[:], g.snew[:])
                  DMA(ssms_o[l], g.snew[:])
                  bank = nextbank()
                  for ft in range(8):
                      yps = PS[:, bank, ft * NS:(ft + 1) * NS]
                      for q4 in range(4):
                          gp = ft * 4 + q4
                          for ri in range(2):
                              hh = q4 // 2
                              MM(yps[hh * 64:(hh + 1) * 64, :], WCz[:, ri, ft, q4, :], g.snb[:, ri, gp, :],
                                 start=(q4 % 2 == 0 and ri == 0), stop=(q4 % 2 == 1 and ri == 1))
                  for ft in range(8):
                      yps = PS[:, bank, ft * NS:(ft + 1) * NS]
                      STT(g.yT[:, ft, :], g.uT[:, ft, :], v8(l, 2)[:, ft:ft + 1], yps, ALU.mult, ALU.add)

              chk(j, l, 3)
              for g in groups:
                  gelu_inplace(g)

              def cons_glu(g, mt, ps):
                  N = g.N
                  ACT(tmpa[:, 0:N], ps, AF.Tanh, bias=bhalf[:, l * 8 + mt:l * 8 + mt + 1], scale=0.5)
                  TS(tmpa[:, 0:N], tmpa[:, 0:N], 0.5, 0.5, ALU.mult, ALU.add)
                  TT(g.hT[:, 8 + mt, :], tmpa[:, 0:N], g.yT[:, mt, :], ALU.mult)

              w_glu_v = w_glu_d[l].rearrange("(k p) c -> p k c", p=128)
              linear(groups, w_glu_v, 8, 1024, lambda g, kc: g.yT[:, kc, :], cons_glu)

              chk(j, l, 4)
              for g in groups:
                  rmsnorm(g, g.hT[:, 0:8, :], 8, v8(l, 0), g.hT[:, 0:8, :], 1024.0)
                  rmsnorm(g, g.hT[:, 8:16, :], 8, v8(l, 1), g.hT[:, 8:16, :], 1024.0)

              def cons_res(g, mt, ps):
                  TT(g.xT[:, mt, :], ps, g.xT[:, mt, :], ALU.add)

              w_out_v = w_out_d[l].rearrange("(k p) c -> p k c", p=128)
              linear(groups, w_out_v, NCH, D, lambda g, kc: g.hT[:, kc, :], cons_res)

              chk(j, l, 5)
              for g in groups:
                  rmsnorm(g, g.xT, NCH, v16(l, 1), g.hT, float(D))
              w_up_v = w_up_d[l].rearrange("(k p) c -> p k c", p=128)
              w_dn_v = w_down_d[l].rearrange("(k p) c -> p k c", p=128)

              def conv_tile(g, i, which, ps, ext, dst):
                  N = g.N
                  ti = i + which * 44
                  if g is GP:
                      CP(ext[:, 0:2], tails[:, l, ti, :])
                      ACT(ext[:, 2:2 + N], ps, AF.Copy)
                      CP(tails[:, l, ti, :], ext[:, N:N + 2])
                      x0, x1, x2 = ext[:, 0:N], ext[:, 1:N + 1], ext[:, 2:N + 2]
                  else:
                      ACT(ext[:, 0:N], ps, AF.Copy)
                      CP(g.cout[:, ti, 0, :], g.sconv[:, ti, 1, :])
                      CP(g.cout[:, ti, 1, :], ext[:, 0:N])
                      x0, x1, x2 = g.sconv[:, ti, 0, :], g.sconv[:, ti, 1, :], ext[:, 0:N]
                  TS(dst[:, 0:N], x0, convp[:, 0, ti:ti + 1], convp[:, 3, ti:ti + 1], ALU.mult, ALU.add)
                  STT(dst[:, 0:N], x1, convp[:, 1, ti:ti + 1], dst[:, 0:N], ALU.mult, ALU.add)
                  STT(dst[:, 0:N], x2, convp[:, 2, ti:ti + 1], dst[:, 0:N], ALU.mult, ALU.add)

              for hf in range(2):
                  for i2 in range(11):
                      i0 = hf * 22 + i2 * 2
                      t = wtile(None, NCH, 0, 512, parts=[(w_up_v, i0 * 128, 256), (w_up_v, DFF + i0 * 128, 256)])
                      for m in range(2):
                          i = i0 + m
                          for g in groups:
                              N = g.N
                              psg = PS[:, nextbank(), 0:N]
                              for kc in range(NCH):
                                  MM(psg, t[:, kc, m * 128:(m + 1) * 128], g.hT[:, kc, :], start=(kc == 0), stop=(kc == NCH - 1))
                              psv = PS[:, nextbank(), 0:N]
                              for kc in range(NCH):
                                  MM(psv, t[:, kc, 256 + m * 128:256 + (m + 1) * 128], g.hT[:, kc, :], start=(kc == 0),
                                     stop=(kc == NCH - 1))
                              conv_tile(g, i, 0, psg, extg, cg)
                              conv_tile(g, i, 1, psv, extv, cvv)
                              ACT(tmpa[:, 0:N], cg[:, 0:N], AF.Tanh, scale=0.5)
                              TS(tmpa[:, 0:N], tmpa[:, 0:N], 0.5, 0.5, ALU.mult, ALU.add)
                              TT(tmpa[:, 0:N], tmpa[:, 0:N], cg[:, 0:N], ALU.mult)
                              dsta = actT[:, i - hf * 22, 0:N] if g is GP else g.yT_act[:, i - hf * 22, :]
                              TT(dsta, tmpa[:, 0:N], cvv[:, 0:N], ALU.mult)
                  dn_view = w_dn_v[:, hf * 22:(hf + 1) * 22, :]
                  linear(groups, dn_view, 22, D,
                         lambda g, kc: (actT[:, kc, :] if g is GP else g.yT_act[:, kc, :]), cons_res, cwmax=256)
              if last:
                  DMA(convp_o[l], tails[:, l, :, :])
              if j == 0 and not NOSAMP:
                  DMA(convs_o[l], GS.cout[:])
          chk(j, L, 0)
          for g in groups:
              N = g.N
              pst = PS[:, 4, 0:N]
              for c in range(NCH):
                  sq = sqs[c % 2][:, 0:N]
                  ACT(sq, g.xT[:, c, :], AF.Square)
                  MM(pst, ones_b[:], sq, start=(c == 0), stop=(c == NCH - 1))
              ACT(stdt[:, 0:N], pst, AF.Sqrt, bias=eps_t[:, 0:1], scale=1.0 / D)
              RECIP(rstd[:, 0:N], stdt[:, 0:N])
              for c in range(NCH):
                  ob = tmpa if c % 2 == 0 else tmpb
                  STT(ob[:, 0:N], g.xT[:, c, :], v16(L, 0)[:, c:c + 1], rstd[:, 0:N], ALU.mult, ALU.mult)
                  if g is GP:
                      DMA(yT_v[:, c, t0:t0 + NT], ob[:, 0:N])
                  else:
                      DMA(ysT_v[:, c, :], ob[:, 0:N])

    try:
        main_loop()
    except _Stop:
        pass
    if DBG:
        def dump(name, ap2d, dt):
            shp = [int(x) for x in ap2d.shape]
            d = nc.dram_tensor("dbg_" + name, shp, dt, kind="ExternalOutput").ap()
            DMA(d, ap2d)
        for gname, g in (("p", GP), ("s", GS)):
            dump(gname + "_xT", g.xT[:].rearrange("p c n -> p (c n)"), F32)
            dump(gname + "_hT", g.hT[:].rearrange("p c n -> p (c n)"), BF16)
            dump(gname + "_qT", g.qT[:].rearrange("p c n -> p (c n)"), BF16)
            dump(gname + "_uT", g.uT[:].rearrange("p c n -> p (c n)"), BF16)
            dump(gname + "_krot", g.krot[:], F32)
            dump(gname + "_vTf", g.vTf[:], F32)
        dump("p_kT", GP.kT[:], BF16)
        dump("p_Vt", GP.Vt[:].rearrange("p a b -> p (a b)"), BF16)
        dump("s_kT", GS.kT[:].rearrange("p a b -> p (a b)"), BF16)
        dump("s_Vt", GS.Vt[:].rearrange("p a b c -> p (a b c)"), BF16)
        dump("arena", arena[:], F32)
        dump("WBc", ZA[:].rearrange("p a b -> p (a b)"), BF16)
        dump("WCc", ZB[:].rearrange("p a b -> p (a b)"), BF16)
        dump("Aco", Aco[:].rearrange("p a b c -> p (a b c)"), F32)
        dump("Bco", Bco[:].rearrange("p a b c -> p (a b c)"), F32)
        dump("s_snew", GS.snew[:].rearrange("p a b c -> p (a b c)"), F32)
    S.emit(stack)
    stack.close()
    return nc


_NC_CACHE = {}


def _feat(v, n):
    return np.ascontiguousarray(np.asarray(v).reshape(n, 128).T)


def prep(inp):
    f32 = np.float32
    g = {k: np.asarray(v) for k, v in inp.items()}
    perm_heads = [h for m in range(8) for h in (m, 8 + m)]
    qperm = np.concatenate([np.arange(h * 64, (h + 1) * 64) for h in perm_heads])
    w_in = g["w_in"].astype(f32).copy()
    w_in[:, :, :1024] = g["w_in"][:, :, qperm]
    w_out = g["w_out"].astype(f32).copy()
    w_out[:, :1024, :] = g["w_out"][:, qperm, :]
    aog = g["attn_out_norm_g"][:, qperm]
    sinks_p = g["attn_sinks"][:, perm_heads]

    vec16 = np.zeros((128, (2 * L + 1) * 16), f32)
    for l in range(L):
        vec16[:, (l * 2) * 16:(l * 2 + 1) * 16] = _feat(g["attn_norm_g"][l], 16)
        vec16[:, (l * 2 + 1) * 16:(l * 2 + 2) * 16] = _feat(g["ffn_norm_g"][l], 16)
    vec16[:, (2 * L) * 16:(2 * L + 1) * 16] = _feat(g["final_norm_g"], 16)
    vec8 = np.zeros((128, L * 4 * 8), f32)
    for l in range(L):
        vec8[:, (l * 4 + 0) * 8:(l * 4 + 1) * 8] = _feat(aog[l], 8)
        vec8[:, (l * 4 + 1) * 8:(l * 4 + 2) * 8] = _feat(g["ssm_out_norm_g"][l], 8)
        vec8[:, (l * 4 + 2) * 8:(l * 4 + 3) * 8] = _feat(g["ssm_d"][l], 8)
        vec8[:, (l * 4 + 3) * 8:(l * 4 + 4) * 8] = _feat(g["b_glu"][l], 8)
    convp = np.zeros((L, 128, 4, 88), f32)
    for l in range(L):
        convp[l, :, 0:3, :] = g["conv_w"][l].reshape(3, 88, 128).transpose(2, 0, 1)
        convp[l, :, 3, :] = g["conv_b"][l].reshape(88, 128).T
    sinks = np.ascontiguousarray(np.broadcast_to(sinks_p.reshape(1, L * 16), (128, L * 16))).astype(f32)

    half = 32
    inv = (np.float32(10000.0) ** (-(np.arange(half, dtype=f32) / np.float32(half)))).astype(f32)
    pidx = np.arange(128)
    sgn = np.where((pidx % 64) >= 32, -1.0, 1.0).astype(f32)

    def rope_tabs(pos):
        ang = pos.astype(f32)[:, None] * inv[None, :]
        c = np.cos(ang).astype(f32)
        s_ = np.sin(ang).astype(f32)
        cosT = np.ascontiguousarray(c[:, pidx % 32].T)
        sinT = np.ascontiguousarray((s_[:, pidx % 32] * sgn[None, :]).T)
        return cosT.astype(f32), sinT.astype(f32)

    cosT, sinT = rope_tabs(np.arange(SEQ))
    coss, sins = rope_tabs(np.full((NS,), 16384))
    ident = np.eye(128, dtype=f32)
    partner = np.where((pidx % 64) < 32, pidx + 32, pidx - 32)
    pswap = np.zeros((128, 128), f32)
    pswap[partner, pidx] = 1.0
    qi = np.arange(128)[:, None]
    kj = np.arange(256)[None, :]
    valid = (kj >= qi) & (kj <= qi + 128)
    maskN = np.where(valid, 0.0, -60000.0).astype(f32)
    mask0 = np.where(valid & (kj >= 128), 0.0, -60000.0).astype(f32)

    rmask = np.zeros((128, 2), f32)
    rmask[:, 0] = ((pidx // 32) % 2 == 0)
    rmask[:, 1] = ((pidx // 32) % 2 == 1)
    bexp = np.zeros((L, 2, 128, 1024), f32)
    pwb = np.zeros((L, 3, 128, 1024), f32)
    cexp = np.zeros((L, 2, 128, 1024), f32)
    pwc = np.zeros((L, 3, 128, 32), f32)
    for l in range(L):
        for ri, key in enumerate(("ssm_b_re", "ssm_b_im")):
            Bt = g[key][l].reshape(8, 4, 2, 64, 16)
            out = np.zeros((4, 2, 16, 8, 2, 64), f32)
            for gl in range(2):
                out[:, gl, :, :, gl, :] = Bt[:, :, gl].transpose(1, 3, 0, 2)
            bexp[l, ri] = out.reshape(128, 1024)
        for ri, key in enumerate(("ssm_c_re", "ssm_c_im")):
            C = g[key][l].reshape(32, 2, 16, 64)
            out = np.zeros((2, 64, 32, 2, 16), f32)
            for gl in range(2):
                out[gl, :, :, gl, :] = C[:, gl].transpose(2, 0, 1)
            cexp[l, ri] = out.reshape(128, 1024)
        params = [g["ssm_a_re"][l], g["ssm_a_im"][l],
                  np.broadcast_to(g["ssm_log_dt"][l][:, None], (G, P))]
        for k, A in enumerate(params):
            A = np.asarray(A, f32)
            a4 = A.reshape(8, 4, 2, 64).transpose(1, 0, 2, 3)
            pwb[l, k] = np.broadcast_to(a4[:, None], (4, 32, 8, 2, 64)).reshape(128, 1024)
            pwc[l, k] = A.reshape(32, 2, 64).transpose(1, 2, 0).reshape(128, 32)

    shared = dict(w_in=w_in, w_glu=np.ascontiguousarray(g["w_glu"], f32), w_out=w_out,
                  w_up=np.ascontiguousarray(g["w_up"], f32), w_down=np.ascontiguousarray(g["w_down"], f32),
                  vec16=vec16, vec8=vec8, convp=convp, sinks=sinks, cosT=cosT, sinT=sinT, coss=coss, sins=sins,
                  ident=ident, pswap=pswap, maskN=maskN, mask0=mask0, rmask=rmask, bexp=bexp, pwb=pwb, cexp=cexp, pwc=pwc)
    in_maps = []
    for c in range(NCORES):
        sq = c % 4
        bs = slice(NS * c, NS * (c + 1))
        m = dict(shared)
        m["xT"] = np.ascontiguousarray(g["x_prompt"][sq].T, f32)
        m["xsT"] = np.ascontiguousarray(g["x_sample"][bs, 0, :].T, f32)
        ck = g["cache_k"][:, bs].reshape(L, NS, 128, 128)
        cv = g["cache_v"][:, bs].reshape(L, NS, 128, 128)
        m["ck"] = np.ascontiguousarray(ck, f32)
        m["ckT"] = np.ascontiguousarray(ck.transpose(0, 1, 3, 2), f32)
        m["cv"] = np.ascontiguousarray(cv, f32)
        h0 = np.zeros((L, 128, 2, 32, NS), f32)
        for ri, key in enumerate(("state_ssm_re", "state_ssm_im")):
            st = g[key][:, bs].reshape(L, NS, 32, 2, 64)
            h0[:, :, ri] = st.transpose(0, 3, 4, 2, 1).reshape(L, 128, 32, NS)
        m["h0"] = h0
        sc = g["state_conv"][:, bs].reshape(L, NS, 2, 88, 128)
        m["sconv"] = np.ascontiguousarray(sc.transpose(0, 4, 3, 2, 1), f32)
        in_maps.append(m)

    return in_maps


def kernel(**inp):
    f32 = np.float32
    in_maps = prep(inp)
    if "nc" not in _NC_CACHE:
        _NC_CACHE["nc"] = build_nc()
    nc = _NC_CACHE["nc"]
    res = run_bass_kernel_spmd(nc, in_maps, core_ids=list(range(NCORES)))
    R = res.results

    y_prompt = np.zeros((4, SEQ, D), f32)
    y_sample = np.zeros((32, 1, D), f32)
    k_prompt = np.zeros((L, 4, 128, 2, 64), f32)
    v_prompt = np.zeros((L, 4, 128, 2, 64), f32)
    sre_p = np.zeros((L, 4, G, P), f32)
    sim_p = np.zeros((L, 4, G, P), f32)
    conv_p = np.zeros((L, 4, 2, 2 * DFF), f32)
    k_sample = np.zeros((L, 32, 128, 2, 64), f32)
    v_sample = np.zeros((L, 32, 128, 2, 64), f32)
    sre_s = np.zeros((L, 32, G, P), f32)
    sim_s = np.zeros((L, 32, G, P), f32)
    conv_s = np.zeros((L, 32, 2, 2 * DFF), f32)
    for c in range(NCORES):
        r = R[c]
        bs = slice(NS * c, NS * (c + 1))
        y_sample[bs, 0, :] = np.asarray(r["ysT"]).T
        k_sample[:, bs] = np.asarray(r["ks"]).reshape(L, NS, 128, 2, 64)
        v_sample[:, bs] = np.asarray(r["vs"]).reshape(L, NS, 128, 2, 64)
        ss = np.asarray(r["ssms"]).reshape(L, 2, 64, 2, 32, NS)
        st = ss.transpose(0, 3, 5, 4, 1, 2).reshape(L, 2, NS, G, P)
        sre_s[:, bs] = st[:, 0]
        sim_s[:, bs] = st[:, 1]
        cs = np.asarray(r["convs"])
        conv_s[:, bs] = cs.transpose(0, 4, 3, 2, 1).reshape(L, NS, 2, 2 * DFF)
        if c < 4:
            y_prompt[c] = np.asarray(r["yT"]).T
            k_prompt[:, c] = np.asarray(r["kp"]).transpose(0, 2, 1).reshape(L, 128, 2, 64)
            v_prompt[:, c] = np.asarray(r["vp"]).transpose(0, 2, 1).reshape(L, 128, 2, 64)
            sp = np.asarray(r["ssmp"]).reshape(L, 2, 64, 2, 32)
            sp = sp.transpose(0, 3, 4, 1, 2).reshape(L, 2, G, P)
            sre_p[:, c] = sp[:, 0]
            sim_p[:, c] = sp[:, 1]
            cp = np.asarray(r["convpo"])
            conv_p[:, c] = cp.transpose(0, 3, 2, 1).reshape(L, 2, 2 * DFF)
    return (y_prompt, y_sample, k_prompt, v_prompt, sre_p, sim_p, conv_p,
            k_sample, v_sample, sre_s, sim_s, conv_s)
```

```python
import math
from contextlib import ExitStack
import numpy as np
import concourse.bass as bass
import concourse.mybir as mybir
from concourse.bass_utils import run_bass_kernel_spmd

F32 = mybir.dt.float32
BF16 = mybir.dt.bfloat16
I32 = mybir.dt.int32
ALU = mybir.AluOpType
AF = mybir.ActivationFunctionType
ESZ = {F32: 4, BF16: 2, I32: 4}

L = 4
D = 2048
NCH = 16
NT = 512
NTILES = 4
SEQ = 2048
NS = 4
DFF = 5632
INC = 2304
G = 64
P = 64
NCORES = 8


MAXMT = None
ASTOP = None
_acount = [0]


def astep():
    _acount[0] += 1
    if ASTOP is not None and _acount[0] >= ASTOP:
        raise _Stop()


NOSAMP = False
LIMIT = None
DBG = False


class _Stop(Exception):
    pass


class Sched:
    K = 8
    ENGMAP = {"pe": "tensor", "dve": "vector", "act": "scalar", "pool": "gpsimd", "sp": "sync"}

    def __init__(self, nc):
        self.nc = nc
        self.streams = {}
        self.lastw = {}
        self.readers = {}
        self.track = set()

    def _keys(self, ap):
        space = str(ap.space)
        apl = list(ap.ap)
        esz = ESZ[ap.dtype]
        name = ap.tensor.name
        if "SB" not in space and "PSUM" not in space:
            if name not in self.track:
                return []
            lo = ap.offset
            hi = lo
            for s, c in apl:
                hi += (c - 1) * abs(s)
            return [(name, g) for g in range(lo * esz // 65536, hi * esz // 65536 + 1)]
        ps = apl[0][0]
        lo = ap.offset % ps if ps > 0 else ap.offset
        hi = lo
        for s, c in apl[1:]:
            hi += (c - 1) * abs(s)
        gran = 2048 if "PSUM" in space else 256
        return [(name, g) for g in range(lo * esz // gran, hi * esz // gran + 1)]

    def add(self, stream, fn, reads, writes, dma=False):
        st = self.streams.setdefault(stream, [])
        seq = len(st)
        deps = {}

        def dep(w):
            if w is None:
                return
            s2, q2 = w
            if s2 == stream and stream == "pe":
                return
            if self.streams[s2][q2]["dma"]:
                deps[(s2, q2)] = q2
            elif deps.get(s2, -1) < q2:
                deps[s2] = q2

        rk = [k for ap in reads for k in self._keys(ap)]
        wk = [k for ap in writes for k in self._keys(ap)]
        for k in rk:
            dep(self.lastw.get(k))
        for k in wk:
            dep(self.lastw.get(k))
            for s2, q2 in self.readers.get(k, {}).items():
                dep((s2, q2))
        for k in rk:
            self.readers.setdefault(k, {})[stream] = seq
        for k in wk:
            self.lastw[k] = (stream, seq)
            self.readers[k] = {}
        st.append(dict(fn=fn, deps=deps, dma=dma, needed=False))

    def emit(self, stack):
        nc = self.nc
        K = self.K
        for st in self.streams.values():
            for op in st:
                for s2, q2 in op["deps"].items():
                    s2 = s2[0] if isinstance(s2, tuple) else s2
                    self.streams[s2][q2]["needed"] = True
        csem = {}
        dsem = {}
        for name, st in self.streams.items():
            c = 0
            d = 0
            for op in st:
                if op["dma"]:
                    op["didx"] = d
                    d += 1
                else:
                    if op["needed"]:
                        c += 1
                    op["cnt"] = c
            if c > 0:
                csem[name] = stack.enter_context(nc.semaphore("c_" + name))
            if d > 0:
                dsem[name] = [stack.enter_context(nc.semaphore("d%d_%s" % (i, name))) for i in range(min(K, d))]
        streams = self.streams

        def run(name, eng):
            seen = {}

            def wait(sem, key, val):
                if seen.get(key, 0) >= val:
                    return
                eng.wait_ge(sem, val)
                seen[key] = val

            for op in streams[name]:
                for s2, q2 in op["deps"].items():
                    s2 = s2[0] if isinstance(s2, tuple) else s2
                    o2 = streams[s2][q2]
                    if o2["dma"]:
                        n = o2["didx"]
                        wait(dsem[s2][n % K], (s2, n % K), 16 * (n // K + 1))
                    else:
                        wait(csem[s2], (s2, "c"), o2["cnt"])
                if op["dma"]:
                    n = op["didx"]
                    if n >= K:
                        wait(dsem[name][n % K], (name, n % K), 16 * (n // K))
                    ins = op["fn"](eng)
                    ins.then_inc(dsem[name][n % K], 16)
                else:
                    ins = op["fn"](eng)
                    if op["needed"]:
                        ins.then_inc(csem[name], 1)
            nd = sum(1 for op in streams[name] if op["dma"])
            if nd > 0:
                for i in range(min(K, nd)):
                    cnt = (nd - i + K - 1) // K
                    wait(dsem[name][i], (name, i), 16 * cnt)

        block = stack.enter_context(nc.Block())
        for name in streams:
            getattr(block, self.ENGMAP[name])(lambda eng, name=name: run(name, eng))


def isap(x):
    return not isinstance(x, (int, float)) and x is not None


def build_nc():
    nc = bass.Bass("TRN2", target_bir_lowering=False)
    S = Sched(nc)
    stack = ExitStack()

    def din(name, shape, dt=F32):
        return nc.dram_tensor(name, list(shape), dt, kind="ExternalInput").ap()

    def dout(name, shape, dt=F32):
        return nc.dram_tensor(name, list(shape), dt, kind="ExternalOutput").ap()

    def sb(name, shape, dt=F32):
        return stack.enter_context(nc.sbuf_tensor("sb_" + name, list(shape), dt))

    xT_d = din("xT", [D, SEQ])
    xsT_d = din("xsT", [D, NS])
    w_in_d = din("w_in", [L, D, INC])
    w_glu_d = din("w_glu", [L, 1024, 1024])
    w_out_d = din("w_out", [L, D, D])
    w_up_d = din("w_up", [L, D, 2 * DFF])
    w_down_d = din("w_down", [L, DFF, D])
    vec16_d = din("vec16", [128, (2 * L + 1) * 16])
    vec8_d = din("vec8", [128, L * 4 * 8])
    convp_d = din("convp", [L, 128, 4, 88])
    sinks_d = din("sinks", [128, L * 16])
    cosT_d = din("cosT", [128, SEQ])
    sinT_d = din("sinT", [128, SEQ])
    coss_d = din("coss", [128, NS])
    sins_d = din("sins", [128, NS])
    ident_d = din("ident", [128, 128])
    pswap_d = din("pswap", [128, 128])
    maskN_d = din("maskN", [128, 256])
    mask0_d = din("mask0", [128, 256])
    rmask_d = din("rmask", [128, 2])
    bexp_d = din("bexp", [L, 2, 128, 1024])
    pwb_d = din("pwb", [L, 3, 128, 1024])
    cexp_d = din("cexp", [L, 2, 128, 1024])
    pwc_d = din("pwc", [L, 3, 128, 32])
    ckT_d = din("ckT", [L, NS, 128, 128])
    ck_d = din("ck", [L, NS, 128, 128])
    cv_d = din("cv", [L, NS, 128, 128])
    h0_d = din("h0", [L, 128, 2, 32, NS])
    sconv_d = din("sconv", [L, 128, 88, 2, NS])

    yT_o = dout("yT", [D, SEQ])
    ysT_o = dout("ysT", [D, NS])
    kp_o = dout("kp", [L, 128, 128])
    vp_o = dout("vp", [L, 128, 128])
    ssmp_o = dout("ssmp", [L, 128, 2, 32])
    convp_o = dout("convpo", [L, 128, 88, 2])
    ks_o = dout("ks", [L, NS, 128, 128])
    vs_o = dout("vs", [L, NS, 128, 128])
    ssms_o = dout("ssms", [L, 128, 2, 32, NS])
    convs_o = dout("convs", [L, 128, 88, 2, NS])

    PS = stack.enter_context(nc.psum_tensor("ps", [128, 7, 512], F32))
    PSB = stack.enter_context(nc.psum_tensor("psb", [128, 1024], BF16))
    Wb = [sb("w%d" % i, [128, 8192], BF16) for i in range(2)]
    ident_f = sb("ident_f", [128, 128])
    ident_b = sb("ident_b", [128, 128], BF16)
    pswap_f = sb("pswap_f", [128, 128])
    pswap_b = sb("pswap_b", [128, 128], BF16)
    ones_b = sb("ones_b", [128, 128], BF16)
    maskN = sb("maskN", [128, 256])
    mask0 = sb("mask0", [128, 256])
    vec16 = sb("vec16", [128, (2 * L + 1) * 16])
    vec8 = sb("vec8", [128, L * 4 * 8])
    bhalf = sb("bhalf", [128, L * 8])
    sinks = sb("sinks", [128, L * 16])
    negsink = sb("negsink", [128, L * 16])
    eps_t = sb("eps_t", [128, 1])
    convp = sb("convp", [128, 4, 88])
    ZA = sb("ZA", [128, 2, 1024], BF16)
    ZB = sb("ZB", [128, 2, 1024], BF16)
    WCz = sb("WCz", [128, 2, 8, 4, 64], BF16)
    rmask = sb("rmask", [128, 2])
    wsc = nc.dram_tensor("wsc", [L, 128, 2, 2048], BF16, kind="Internal").ap()
    S.track.add("wsc")
    Aco = sb("Aco", [128, L, 2, 32])
    Bco = sb("Bco", [128, L, 2, 32])
    kprev = sb("kprev", [128, L, 128], BF16)
    vprev = sb("vprev", [128, L, 128], BF16)
    Scar = sb("Scar", [128, L, 2, 32])
    tails = sb("tails", [128, L, 88, 2])
    sqs = [sb("sq%d" % i, [128, NT], BF16) for i in range(2)]
    stdt = sb("stdt", [128, NT])
    rstd = sb("rstd", [128, NT])
    ropeA = sb("ropeA", [128, NT], BF16)
    ropeB = sb("ropeB", [128, NT], BF16)
    st_rmax = sb("st_rmax", [128, 16])
    st_nb = sb("st_nb", [128, 16])
    st_rsum = sb("st_rsum", [128, 16])
    st_es = sb("st_es", [128, 16])
    st_den = sb("st_den", [128, 16])
    st_rden = sb("st_rden", [128, 16])
    arena = sb("arena", [128, 10304])
    ShL = arena[:, 0:4096].rearrange("p (r g k) -> p r g k", r=2, g=32)
    tmp3 = arena[:, 4096:4352].rearrange("p (g s) -> p g s", s=8)
    lt1 = arena[:, 6144:6656]
    lt2 = arena[:, 6656:7168]
    Ccar = arena[:, 7168:7744].rearrange("p (c r g) -> p c r g", r=2, g=32)
    Ptab = arena[:, 7744:8256].rearrange("p (r g s) -> p r g s", r=2, g=32)
    Shb = arena[:, 8256:10304].bitcast(BF16).rearrange("p (r g k) -> p r g k", r=2, g=32)
    actT = arena[:, 0:5632].bitcast(BF16).rearrange("p (i n) -> p i n", n=NT)
    setA = arena[:, 0:6144].rearrange("p (a w) -> p a w", w=512)
    WBc = arena[:, 8256:9280].bitcast(BF16).rearrange("p (r c) -> p r c", r=2)
    WCc = arena[:, 9280:10304].bitcast(BF16).rearrange("p (r c) -> p r c", r=2)
    Sm = [arena[:, i * 256:(i + 1) * 256] for i in range(2)]
    Pb = [arena[:, 512 + i * 128:512 + (i + 1) * 128].bitcast(BF16) for i in range(2)]
    PTs = [arena[:, 768 + i * 128:768 + (i + 1) * 128].bitcast(BF16) for i in range(2)]
    Otok = arena[:, 1024:1536].bitcast(BF16)
    stg = arena[:, 6144:7168].bitcast(BF16).rearrange("p (a w) -> p a w", w=512)
    extg = arena[:, 5632:5632 + NT + 2]
    extv = arena[:, 6152:6152 + NT + 2]
    cg = arena[:, 6672:6672 + NT]
    cvv = arena[:, 7184:7184 + NT]
    sc_t1 = sb("sc_t1", [128, 2, 32])
    sc_t2 = sb("sc_t2", [128, 2, 32])
    A8 = sb("A8", [128, 2, 32])
    B8 = sb("B8", [128, 2, 32])
    tmpa = sb("tmpa", [128, NT])
    tmpb = sb("tmpb", [128, NT])

    class Grp:
        pass

    def mkgrp(name, N, nunits):
        g = Grp()
        g.name = name
        g.N = N
        g.xT = sb(name + "_xT", [128, NCH, N])
        g.hT = sb(name + "_hT", [128, NCH, N], BF16)
        g.qT = sb(name + "_qT", [128, 8, N], BF16)
        g.uT = sb(name + "_uT", [128, 8, N], BF16)
        g.yT = g.qT
        g.krot = sb(name + "_krot", [128, N])
        g.vTf = sb(name + "_vTf", [128, N])
        g.vTb = sb(name + "_vTb", [128, N], BF16)
        g.cos = sb(name + "_cos", [128, N])
        g.sin = sb(name + "_sin", [128, N])
        return g

    GP = mkgrp("p", NT, 1)
    GS = mkgrp("s", NS, NS)
    GP.kT = sb("p_kT", [128, 128 + NT], BF16)
    GP.Vt = sb("p_Vt", [128, 5, 128], BF16)
    GS.kT = sb("s_kT", [128, NS, 256], BF16)
    GS.Vt = sb("s_Vt", [128, NS, 2, 128], BF16)
    GS.h0 = sb("s_h0", [128, 2, 32, NS])
    GS.sconv = sb("s_sconv", [128, 88, 2, NS])
    GS.snew = sb("s_snew", [128, 2, 32, NS])
    GS.snb = sb("s_snb", [128, 2, 32, NS], BF16)
    GS.cout = sb("s_cout", [128, 88, 2, NS])
    GS.Xs = sb("s_Xs", [128, 2, 32, NS])
    GS.yT_act = sb("s_yTact", [128, 22, NS], BF16)

    def MM(out, lhsT, rhs, start=True, stop=True):
        S.add("pe", lambda e: e.matmul(out, lhsT=lhsT, rhs=rhs, start=start, stop=stop), [lhsT, rhs], [out])

    def TR(out, in_, ident):
        S.add("pe", lambda e: e.transpose(out, in_, ident), [in_, ident], [out])

    def ACT(out, in_, func, bias=None, scale=None, accum=None):
        kw = {}
        rd = [in_]
        wr = [out]
        if bias is not None:
            kw["bias"] = bias
            if isap(bias):
                rd.append(bias)
        if scale is not None:
            kw["scale"] = scale
            if isap(scale):
                rd.append(scale)
        if accum is not None:
            kw["accum_out"] = accum
            wr.append(accum)
        S.add("act", lambda e: e.activation(out=out, in_=in_, func=func, **kw), rd, wr)

    def TT(out, a, b, op, eng="dve"):
        S.add(eng, lambda e: e.tensor_tensor(out=out, in0=a, in1=b, op=op), [a, b], [out])

    def TS(out, a, s1, s2, op0, op1=None, eng="dve"):
        rd = [a] + [x for x in (s1, s2) if isap(x)]
        if op1 is None:
            S.add(eng, lambda e: e.tensor_scalar(out=out, in0=a, scalar1=s1, scalar2=None, op0=op0), rd, [out])
        else:
            S.add(eng, lambda e: e.tensor_scalar(out=out, in0=a, scalar1=s1, scalar2=s2, op0=op0, op1=op1), rd, [out])

    def STT(out, a, sc, b, op0, op1):
        rd = [a, b] + ([sc] if isap(sc) else [])
        S.add("dve", lambda e: e.scalar_tensor_tensor(out=out, in0=a, scalar=sc, in1=b, op0=op0, op1=op1), rd, [out])

    def CP(out, in_, eng="dve"):
        if eng == "act":
            ACT(out, in_, AF.Copy)
        else:
            S.add(eng, lambda e: e.tensor_copy(out=out, in_=in_), [in_], [out])

    def RECIP(out, in_):
        S.add("dve", lambda e: e.reciprocal(out=out, in_=in_), [in_], [out])

    def MEMSET(ap, val):
        S.add("dve", lambda e: e.memset(ap, val), [], [ap])

    def DMA(out, in_, q="sp"):
        S.add(q, lambda e: e.dma_start(out=out, in_=in_), [in_], [out], dma=True)

    def REDMAX(out, in_):
        S.add("dve", lambda e: e.tensor_reduce(out=out, in_=in_, axis=mybir.AxisListType.X, op=ALU.max), [in_], [out])

    def TTR(out, a, b, op0, op1, init, accum):
        S.add("dve", lambda e: e.tensor_tensor_reduce(out=out, in0=a, in1=b, scale=1.0, scalar=init, op0=op0, op1=op1,
                                                     accum_out=accum), [a, b], [out, accum])

    DMA(ident_f[:], ident_d)
    DMA(pswap_f[:], pswap_d)
    DMA(maskN[:], maskN_d)
    DMA(mask0[:], mask0_d)
    DMA(vec16[:], vec16_d)
    DMA(vec8[:], vec8_d)
    DMA(sinks[:], sinks_d)
    DMA(rmask[:], rmask_d)
    CP(ident_b[:], ident_f[:])
    CP(pswap_b[:], pswap_f[:])
    MEMSET(ones_b[:], 1.0)
    MEMSET(eps_t[:], 1e-6)
    TS(negsink[:], sinks[:], -1.0, None, ALU.mult)
    MEMSET(kprev[:], 0.0)
    MEMSET(vprev[:], 0.0)
    MEMSET(Scar[:], 0.0)
    MEMSET(tails[:], 0.0)
    MEMSET(GS.kT[:], 0.0)
    MEMSET(GS.Vt[:], 0.0)
    MEMSET(WCz[:], 0.0)
    for l in range(L):
        TS(bhalf[:, l * 8:(l + 1) * 8], vec8[:, (l * 4 + 3) * 8:(l * 4 + 4) * 8], 0.5, None, ALU.mult)

    def v16(l, which):
        o = (l * 2 + which) * 16
        return vec16[:, o:o + 16]

    def v8(l, which):
        o = (l * 4 + which) * 8
        return vec8[:, o:o + 8]

    TWO_PI = 2.0 * math.pi

    def sincos(dst, theta, shift, W, tmp1, tmp2i, tmp3):
        TS(tmp1, theta, 1.0 / TWO_PI, shift, ALU.mult, ALU.add)
        CP(tmp2i, tmp1)
        CP(tmp3, tmp2i)
        TT(tmp1, tmp1, tmp3, ALU.subtract)
        TS(tmp3, tmp1, 0.5, None, ALU.is_gt)
        TT(tmp1, tmp1, tmp3, ALU.subtract)
        TS(tmp3, tmp1, -0.5, None, ALU.is_lt)
        TT(tmp1, tmp1, tmp3, ALU.add)
        ACT(dst, tmp1, AF.Sin, scale=6.283185)

    def lam_q(W, ar, ai, ldt, lr, li, qr, qi, t1, t2i, t3, t4):
        ACT(ldt, ldt, AF.Exp)
        TT(t4, ai, ldt, ALU.mult)
        sincos(li, t4, 0.0, W, t1, t2i, t3)
        sincos(lr, t4, 0.25, W, t1, t2i, t3)
        TT(t4, ar, ldt, ALU.mult)
        ACT(t4, t4, AF.Exp)
        TT(lr, lr, t4, ALU.mult)
        TT(li, li, t4, ALU.mult)
        TT(t1, ar, ar, ALU.mult)
        TT(t3, ai, ai, ALU.mult)
        TT(t1, t1, t3, ALU.add)
        RECIP(t1, t1)
        TS(t4, lr, -1.0, None, ALU.add)
        TT(qr, t4, ar, ALU.mult)
        TT(t3, li, ai, ALU.mult)
        TT(qr, qr, t3, ALU.add)
        TT(qr, qr, t1, ALU.mult)
        TT(qi, li, ar, ALU.mult)
        TT(t3, t4, ai, ALU.mult)
        TT(qi, qi, t3, ALU.subtract)
        TT(qi, qi, t1, ALU.mult)

    for l in range(L):
        for hv in range(2):
            hs = slice(hv * 512, (hv + 1) * 512)
            ar, ai, ldt = setA[:, 0, :], setA[:, 1, :], setA[:, 2, :]
            lr, li, qr, qi = setA[:, 3, :], setA[:, 4, :], setA[:, 5, :], setA[:, 6, :]
            t1, t3, t4 = setA[:, 7, :], setA[:, 8, :], setA[:, 9, :]
            t2i = setA[:, 10, :].bitcast(I32)
            bre, bim = setA[:, 10, :], setA[:, 11, :]
            DMA(ar, pwb_d[l, 0, :, hs])
            DMA(ai, pwb_d[l, 1, :, hs])
            DMA(ldt, pwb_d[l, 2, :, hs])
            lam_q(512, ar, ai, ldt, lr, li, qr, qi, t1, t2i, t3, t4)
            DMA(bre, bexp_d[l, 0, :, hs])
            DMA(bim, bexp_d[l, 1, :, hs])
            TT(t1, qr, bre, ALU.mult)
            TT(t3, qi, bim, ALU.mult)
            TT(stg[:, 0, :], t1, t3, ALU.subtract)
            TT(t1, qr, bim, ALU.mult)
            TT(t3, qi, bre, ALU.mult)
            TT(stg[:, 1, :], t1, t3, ALU.add)
            cre, cim = setA[:, 10, :], setA[:, 11, :]
            DMA(cre, cexp_d[l, 0, :, hs])
            DMA(cim, cexp_d[l, 1, :, hs])
            CP(stg[:, 2, :], cre)
            TS(stg[:, 3, :], cim, -1.0, None, ALU.mult)
            DMA(wsc[l, :, 0, hv * 512:(hv + 1) * 512], stg[:, 0, :])
            DMA(wsc[l, :, 0, 1024 + hv * 512:1024 + (hv + 1) * 512], stg[:, 1, :])
            DMA(wsc[l, :, 1, hv * 512:(hv + 1) * 512], stg[:, 2, :])
            DMA(wsc[l, :, 1, 1024 + hv * 512:1024 + (hv + 1) * 512], stg[:, 3, :])
        sar, sai, sdt = setA[:, 0, 0:32], setA[:, 1, 0:32], setA[:, 2, 0:32]
        slr, sli, sqr, sqi = setA[:, 3, 0:32], setA[:, 4, 0:32], setA[:, 5, 0:32], setA[:, 6, 0:32]
        s1, s3, s4 = setA[:, 7, 0:32], setA[:, 8, 0:32], setA[:, 9, 0:32]
        s2i = setA[:, 10, 0:32].bitcast(I32)
        DMA(sar, pwc_d[l, 0])
        DMA(sai, pwc_d[l, 1])
        DMA(sdt, pwc_d[l, 2])
        lam_q(32, sar, sai, sdt, slr, sli, sqr, sqi, s1, s2i, s3, s4)
        CP(Aco[:, l, 0, :], slr)
        CP(Aco[:, l, 1, :], slr)
        TS(Bco[:, l, 0, :], sli, -1.0, None, ALU.mult)
        CP(Bco[:, l, 1, :], sli)

    state = dict(wi=0, bank=0)

    def wtile(view, KC, c0, cw, parts=None):
        buf = Wb[state["wi"] % 2]
        state["wi"] += 1
        t = buf[:, 0:KC * cw].rearrange("p (k c) -> p k c", c=cw)
        if parts is None:
            DMA(t, view[:, :, c0:c0 + cw], q="pool")
        else:
            o = 0
            for (v2, a0, aw) in parts:
                DMA(t[:, :, o:o + aw], v2[:, :, a0:a0 + aw], q="pool")
                o += aw
        return t

    def nextbank():
        b = state["bank"]
        state["bank"] = (b + 1) % 4
        return b

    def linear(groups, view, KC, ncols, rhs_of, consume, cwmax=512):
        c0 = 0
        while c0 < ncols:
            cw = min(cwmax, ncols - c0)
            t = wtile(view, KC, c0, cw)
            for m in range(cw // 128):
                mt = c0 // 128 + m
                if MAXMT is not None and mt >= MAXMT:
                    raise _Stop()
                for g in groups:
                    ps = PS[:, nextbank(), 0:g.N]
                    for kc in range(KC):
                        MM(ps, t[:, kc, m * 128:(m + 1) * 128], rhs_of(g, kc), start=(kc == 0), stop=(kc == KC - 1))
                    consume(g, mt, ps)
            c0 += cw

    def rmsnorm(g, src, nch, gain, dst, Dn):
        N = g.N
        pst = PS[:, 4, 0:N]
        for c in range(nch):
            sq = sqs[c % 2][:, 0:N]
            ACT(sq, src[:, c, :], AF.Square)
            MM(pst, ones_b[:], sq, start=(c == 0), stop=(c == nch - 1))
        ACT(stdt[:, 0:N], pst, AF.Sqrt, bias=eps_t[:, 0:1], scale=1.0 / Dn)
        RECIP(rstd[:, 0:N], stdt[:, 0:N])
        for c in range(nch):
            STT(dst[:, c, :], src[:, c, :], gain[:, c:c + 1], rstd[:, 0:N], ALU.mult, ALU.mult)

    def rope(g, ps, dst_bf, dst_f32=None):
        N = g.N
        TT(ropeA[:, 0:N], ps, g.cos[:, 0:N], ALU.mult)
        TT(ropeB[:, 0:N], ps, g.sin[:, 0:N], ALU.mult)
        ps2 = PS[:, 5, 0:N]
        MM(ps2, ident_b[:], ropeA[:, 0:N], start=True, stop=False)
        MM(ps2, pswap_b[:], ropeB[:, 0:N], start=False, stop=True)
        ACT(dst_bf, ps2, AF.Copy)
        if dst_f32 is not None:
            import os
            kv = os.environ.get("KVAR", "a")
            if kv == "a":
                ACT(dst_f32, ps2, AF.Copy)
            elif kv == "b":
                CP(dst_f32, ps2)
            else:
                pass

    def attn_unit(l, nq, qap_of, kT2, Vblk, mask, acols, g):
        for m in range(8):
            for e in range(2):
                hi = m * 2 + e
                col = l * 16 + hi
                Sps = PS[0:nq, 6 if e == 0 else 4, 0:256]
                MM(Sps, qap_of(m, e), kT2[e * 64:(e + 1) * 64, :])
                astep()
                TT(Sm[e][0:nq, :], Sps, mask[0:nq, :], ALU.add)
                astep()
                REDMAX(st_rmax[0:nq, hi:hi + 1], Sm[e][0:nq, :])
                astep()
                TS(st_nb[0:nq, hi:hi + 1], st_rmax[0:nq, hi:hi + 1], -0.125, negsink[0:nq, col:col + 1], ALU.mult, ALU.min)
                astep()
                ACT(Pb[e][0:nq, :], Sm[e][0:nq, :], AF.Exp, bias=st_nb[0:nq, hi:hi + 1], scale=0.125,
                    accum=st_rsum[0:nq, hi:hi + 1])
                astep()
                ACT(st_es[0:nq, hi:hi + 1], sinks[0:nq, col:col + 1], AF.Exp, bias=st_nb[0:nq, hi:hi + 1], scale=1.0)
                astep()
                TT(st_den[0:nq, hi:hi + 1], st_rsum[0:nq, hi:hi + 1], st_es[0:nq, hi:hi + 1], ALU.add)
                astep()
                RECIP(st_rden[0:nq, hi:hi + 1], st_den[0:nq, hi:hi + 1])
                astep()
                for kb in range(2):
                    TR(PSB[:, e * 256 + kb * 128:e * 256 + kb * 128 + nq], Pb[e][0:nq, kb * 128:(kb + 1) * 128],
                       ident_b[0:nq, 0:nq])
                    astep()
                for kb in range(2):
                    CP(PTs[e][:, kb * 128:kb * 128 + nq], PSB[:, e * 256 + kb * 128:e * 256 + kb * 128 + nq])
                    astep()
                Ops = PS[0:nq, 5, (hi % 8) * 64:(hi % 8 + 1) * 64]
                for kb in range(2):
                    MM(Ops, PTs[e][:, kb * 128:kb * 128 + nq], Vblk[kb][:, e * 64:(e + 1) * 64], start=(kb == 0), stop=(kb == 1))
                    astep()
                ACT(Otok[0:nq, hi * 64:(hi + 1) * 64], Ops, AF.Copy, scale=st_rden[0:nq, hi:hi + 1])
                astep()
        for m in range(8):
            o = 512 + (m % 4) * 128
            TR(PSB[:, o:o + nq], Otok[0:nq, m * 128:(m + 1) * 128], ident_b[0:nq, 0:nq])
            astep()
            CP(g.hT[:, m, acols], PSB[:, o:o + nq])
            astep()

    def gelu_inplace(g):
        N = g.N
        for c in range(8):
            y = g.yT[:, c, :]
            TT(tmpa[:, 0:N], y, y, ALU.mult)
            TS(tmpa[:, 0:N], tmpa[:, 0:N], 0.044715, 1.0, ALU.mult, ALU.add)
            TT(tmpa[:, 0:N], tmpa[:, 0:N], y, ALU.mult)
            ACT(tmpb[:, 0:N], tmpa[:, 0:N], AF.Tanh, scale=0.7978845608028654)
            TS(tmpb[:, 0:N], tmpb[:, 0:N], 0.5, 0.5, ALU.mult, ALU.add)
            TT(y, tmpb[:, 0:N], y, ALU.mult)

    xT_v = xT_d.rearrange("(c p) t -> p c t", p=128)
    yT_v = yT_o.rearrange("(c p) t -> p c t", p=128)
    xsT_v = xsT_d.rearrange("(c p) t -> p c t", p=128)
    ysT_v = ysT_o.rearrange("(c p) t -> p c t", p=128)

    def chk(j, l, p):
        if LIMIT is not None and (j, l, p) >= tuple(LIMIT):
            raise _Stop()

    def main_loop():
      for j in range(NTILES):
          t0 = j * NT
          groups = [GP] + ([GS] if (j == 0 and not NOSAMP) else [])
          chk(j, -1, 0)
          for c4 in range(4):
              DMA(GP.xT[:, c4 * 4:(c4 + 1) * 4, :], xT_v[:, c4 * 4:(c4 + 1) * 4, t0:t0 + NT])
          DMA(GP.cos[:], cosT_d[:, t0:t0 + NT])
          DMA(GP.sin[:], sinT_d[:, t0:t0 + NT])
          if j == 0 and not NOSAMP:
              DMA(GS.xT[:], xsT_v)
              DMA(GS.cos[:], coss_d)
              DMA(GS.sin[:], sins_d)
          for l in range(L):
              last = (j == NTILES - 1)
              chk(j, l, 0)
              for g in groups:
                  rmsnorm(g, g.xT, NCH, v16(l, 0), g.hT, float(D))
              chk(j, l, 0.1)
              CP(GP.kT[:, 0:128], kprev[:, l, :])
              CP(GP.Vt[:, 0, :], vprev[:, l, :])
              if j == 0 and not NOSAMP:
                  for b in range(NS):
                      DMA(GS.kT[:, b, 0:128], ckT_d[l, b], q="pool")
                      DMA(GS.Vt[:, b, 0, :], cv_d[l, b], q="pool")
                  chk(j, l, 0.12)
                  DMA(GS.h0[:], h0_d[l])
                  DMA(GS.sconv[:], sconv_d[l])
              DMA(convp[:], convp_d[l])
              chk(j, l, 0.13)
              DMA(WBc.rearrange("p r c -> p (r c)"), wsc[l, :, 0, :])
              DMA(WCc.rearrange("p r c -> p (r c)"), wsc[l, :, 1, :])
              for r_ in range(2):
                  TS(ZA[:, r_, :], WBc[:, r_, :], rmask[:, 0:1], None, ALU.mult)
                  TS(ZB[:, r_, :], WBc[:, r_, :], rmask[:, 1:2], None, ALU.mult)
              WCv = WCc.rearrange("p r (f q c) -> p r f q c", f=8, q=4)
              for r_ in range(2):
                  for q_ in range(4):
                      CP(WCz[:, r_, :, q_, (q_ % 2) * 32:(q_ % 2) * 32 + 32], WCv[:, r_, :, q_, :])

              lrT, liT = Aco[:, l, 0, :], Bco[:, l, 1, :]
              CP(Ptab[:, 0, :, 0], lrT)
              CP(Ptab[:, 1, :, 0], liT)
              for s_ in range(1, 8):
                  pr, pi_ = Ptab[:, 0, :, s_ - 1], Ptab[:, 1, :, s_ - 1]
                  TT(sc_t1[:, 0, :], pr, lrT, ALU.mult)
                  TT(sc_t1[:, 1, :], pi_, liT, ALU.mult)
                  TT(Ptab[:, 0, :, s_], sc_t1[:, 0, :], sc_t1[:, 1, :], ALU.subtract)
                  TT(sc_t2[:, 0, :], pr, liT, ALU.mult)
                  TT(sc_t2[:, 1, :], pi_, lrT, ALU.mult)
                  TT(Ptab[:, 1, :, s_], sc_t2[:, 0, :], sc_t2[:, 1, :], ALU.add)
              CP(A8[:, 0, :], Ptab[:, 0, :, 7])
              CP(A8[:, 1, :], Ptab[:, 0, :, 7])
              TS(B8[:, 0, :], Ptab[:, 1, :, 7], -1.0, None, ALU.mult)
              CP(B8[:, 1, :], Ptab[:, 1, :, 7])
              chk(j, l, 0.2)
              def cons_in(g, mt, ps):
                  N = g.N
                  if mt < 8:
                      rope(g, ps, g.qT[:, mt, :])
                  elif mt == 8:
                      if g is GP:
                          rope(g, ps, g.kT[:, 128:128 + N], g.krot[:, 0:N])
                      else:
                          rope(g, ps, g.kT[:, :, 128:129].rearrange("p b o -> p (b o)"), g.krot[:, 0:N])
                  elif mt == 9:
                      ACT(g.vTb[:, 0:N], ps, AF.Copy)
                      ACT(g.vTf[:, 0:N], ps, AF.Copy)
                  else:
                      ACT(g.uT[:, mt - 10, :], ps, AF.Copy)

              w_in_v = w_in_d[l].rearrange("(k p) c -> p k c", p=128)
              linear(groups, w_in_v, NCH, INC, lambda g, kc: g.hT[:, kc, :], cons_in)

              chk(j, l, 0.3)
              for blk in range(4):
                  o = (blk % 4) * 128
                  TR(PSB[:, 512 + o:512 + o + 128], GP.vTb[:, blk * 128:(blk + 1) * 128], ident_b[:])
                  CP(GP.Vt[:, blk + 1, :], PSB[:, 512 + o:512 + o + 128])
              chk(j, l, 0.4)
              CP(kprev[:, l, :], GP.kT[:, NT:NT + 128])
              CP(vprev[:, l, :], GP.Vt[:, 4, :])
              if last:
                  DMA(kp_o[l], GP.krot[:, NT - 128:NT])
                  DMA(vp_o[l], GP.vTf[:, NT - 128:NT])
              if j == 0 and not NOSAMP:
                  for b in range(NS):
                      MM(PS[0:1, 4, 0:128], GS.vTb[:, b:b + 1], ident_b[:])
                      CP(GS.Vt[0:1, b, 1, :], PS[0:1, 4, 0:128])
                      DMA(ks_o[l, b, 0:127, :], ck_d[l, b, 1:128, :])
                      DMA(vs_o[l, b, 0:127, :], cv_d[l, b, 1:128, :])
                      DMA(ks_o[l, b, 127, :].rearrange("(p o) -> p o", o=1), GS.krot[:, b:b + 1])
                      DMA(vs_o[l, b, 127, :].rearrange("(p o) -> p o", o=1), GS.vTf[:, b:b + 1])

              chk(j, l, 1)
              for qb in range(4):
                  mask = mask0 if (j == 0 and qb == 0) else maskN
                  attn_unit(l, 128,
                            lambda m, e, qb=qb: GP.qT[e * 64:(e + 1) * 64, m, qb * 128:(qb + 1) * 128],
                            GP.kT[:, qb * 128:qb * 128 + 256],
                            [GP.Vt[:, qb, :], GP.Vt[:, qb + 1, :]],
                            mask, slice(qb * 128, (qb + 1) * 128), GP)
              if j == 0 and not NOSAMP:
                  for b in range(NS):
                      attn_unit(l, 1,
                                lambda m, e, b=b: GS.qT[e * 64:(e + 1) * 64, m, b:b + 1],
                                GS.kT[:, b, :],
                                [GS.Vt[:, b, 0, :], GS.Vt[:, b, 1, :]],
                                maskN, slice(b, b + 1), GS)

              chk(j, l, 2)
              CP(Ccar[:, 0, :, :], Scar[:, l, :, :])
              for stt in range(8):
                  cs = slice(stt * 64, (stt + 1) * 64)
                  Xv = ShL.rearrange("p r (t q) k -> p r t (q k)", q=4)
                  for ri in range(2):
                      for bb in range(2):
                          for hh in range(2):
                              bank = 2 * hh + bb
                              for i8 in range(8):
                                  t = bb * 4 + i8 // 2
                                  q4 = 2 * hh + i8 % 2
                                  Z = ZA if q4 % 2 == 0 else ZB
                                  MM(PS[:, bank, i8 * 64:(i8 + 1) * 64],
                                     Z[hh * 64:(hh + 1) * 64, ri, t * 128:(t + 1) * 128],
                                     GP.uT[hh * 64:(hh + 1) * 64, t, cs])
                          for hh in range(2):
                              bank = 2 * hh + bb
                              ACT(Xv[:, ri, bb * 4:(bb + 1) * 4, hh * 128:(hh + 1) * 128],
                                  PS[:, bank, :].rearrange("p (a b) -> p a b", b=128), AF.Copy)
                  if stt == 0:
                      chk(j, l, 2.1)
                  if stt > 0:
                      CP(Ccar[:, 0, :, :], Ccar[:, 8, :, :])
                  Lv = ShL.rearrange("p r g (c s) -> p (r g) c s", s=8)
                  Lr = [ShL[:, r_, :, :].rearrange("p g (c s) -> p g c s", s=8) for r_ in range(2)]
                  Aflat = Aco[:, l, :, :].rearrange("p r g -> p (r g)").unsqueeze(2).broadcast_to([128, 64, 8])
                  Bh = [Bco[:, l, r_, :].unsqueeze(2).broadcast_to([128, 32, 8]) for r_ in range(2)]
                  t1v = lt1.rearrange("p (a c) -> p a c", c=8)
                  t2v = lt2.rearrange("p (r g c) -> p r g c", r=2, c=8)
                  for s_ in range(1, 8):
                      TT(t1v, Aflat, Lv[:, :, :, s_ - 1], ALU.mult)
                      TT(t2v[:, 0, :, :], Bh[0], Lr[1][:, :, :, s_ - 1], ALU.mult)
                      TT(t2v[:, 1, :, :], Bh[1], Lr[0][:, :, :, s_ - 1], ALU.mult)
                      TT(lt1, lt1, lt2, ALU.add)
                      TT(Lv[:, :, :, s_], Lv[:, :, :, s_], t1v, ALU.add)
                  for c_ in range(8):
                      TT(sc_t1[:], A8[:], Ccar[:, c_, :, :], ALU.mult)
                      TT(sc_t2[:, 0, :], B8[:, 0, :], Ccar[:, c_, 1, :], ALU.mult)
                      TT(sc_t2[:, 1, :], B8[:, 1, :], Ccar[:, c_, 0, :], ALU.mult)
                      TT(sc_t1[:], sc_t1[:], sc_t2[:], ALU.add)
                      TT(Ccar[:, c_ + 1, :, :].rearrange("p r g -> p (r g)"), sc_t1[:].rearrange("p r g -> p (r g)"),
                         Lv[:, :, c_, 7], ALU.add)
                  for c_ in range(8):
                      cre = Ccar[:, c_, 0, :].unsqueeze(2).broadcast_to([128, 32, 8])
                      cim = Ccar[:, c_, 1, :].unsqueeze(2).broadcast_to([128, 32, 8])
                      Lre, Lim = Lr[0][:, :, c_, :], Lr[1][:, :, c_, :]
                      TT(tmp3, Ptab[:, 0, :, :], cre, ALU.mult)
                      TT(Lre, Lre, tmp3, ALU.add)
                      TT(tmp3, Ptab[:, 1, :, :], cim, ALU.mult)
                      TT(Lre, Lre, tmp3, ALU.subtract)
                      TT(tmp3, Ptab[:, 0, :, :], cim, ALU.mult)
                      TT(Lim, Lim, tmp3, ALU.add)
                      TT(tmp3, Ptab[:, 1, :, :], cre, ALU.mult)
                      TT(Lim, Lim, tmp3, ALU.add)
                  if stt == 0:
                      chk(j, l, 2.2)
                  for r_ in range(2):
                      CP(Shb[:, r_, :, :], ShL[:, r_, :, :])
                  if stt == 0:
                      chk(j, l, 2.3)
                  for ft in range(8):
                      if ft % 8 == 0:
                          bank = nextbank()
                      yps = PS[:, bank, (ft % 8) * 64:(ft % 8 + 1) * 64]
                      for q4 in range(4):
                          gp = ft * 4 + q4
                          for ri in range(2):
                              hh = q4 // 2
                              MM(yps[hh * 64:(hh + 1) * 64, :], WCz[:, ri, ft, q4, :], Shb[:, ri, gp, :],
                                 start=(q4 % 2 == 0 and ri == 0), stop=(q4 % 2 == 1 and ri == 1))
                  for ft in range(8):
                      yps = PS[:, bank, (ft % 8) * 64:(ft % 8 + 1) * 64]
                      STT(GP.yT[:, ft, cs], GP.uT[:, ft, cs], v8(l, 2)[:, ft:ft + 1], yps, ALU.mult, ALU.add)
                  if stt == 0:
                      chk(j, l, 2.4)
              CP(Scar[:, l, :, :], Ccar[:, 8, :, :])
              chk(j, l, 2.5)
              if last:
                  DMA(ssmp_o[l], Ccar[:, 8, :, :])
              if j == 0 and not NOSAMP:
                  g = GS
                  Xvs = g.Xs[:].rearrange("p r (t q) k -> p r t (q k)", q=4)
                  for ri in range(2):
                      for bb in range(2):
                          for hh in range(2):
                              bank = 2 * hh + bb
                              for i8 in range(8):
                                  t = bb * 4 + i8 // 2
                                  q4 = 2 * hh + i8 % 2
                                  Z = ZA if q4 % 2 == 0 else ZB
                                  MM(PS[:, bank, i8 * NS:(i8 + 1) * NS],
                                     Z[hh * 64:(hh + 1) * 64, ri, t * 128:(t + 1) * 128],
                                     g.uT[hh * 64:(hh + 1) * 64, t, :])
                          for hh in range(2):
                              bank = 2 * hh + bb
                              CP(Xvs[:, ri, bb * 4:(bb + 1) * 4, hh * 2 * NS:(hh + 1) * 2 * NS],
                                 PS[:, bank, 0:8 * NS].rearrange("p (a b) -> p a b", b=2 * NS))
                  for b in range(NS):
                      TT(sc_t1[:], Aco[:, l, :, :], g.h0[:, :, :, b], ALU.mult)
                      TT(sc_t2[:, 0, :], Bco[:, l, 0, :], g.h0[:, 1, :, b], ALU.mult)
                      TT(sc_t2[:, 1, :], Bco[:, l, 1, :], g.h0[:, 0, :, b], ALU.mult)
                      TT(sc_t1[:], sc_t1[:], sc_t2[:], ALU.add)
                      TT(g.snew[:, :, :, b], sc_t1[:], g.Xs[:, :, :, b], ALU.add)
                  CP(g.snb[:], g.snew[:])
                  DMA(ssms_o[l], g.snew[:])
                  bank = nextbank()
                  for ft in range(8):
                      yps = PS[:, bank, ft * NS:(ft + 1) * NS]
                      for q4 in range(4):
                          gp = ft * 4 + q4
                          for ri in range(2):
                              hh = q4 // 2
                              MM(yps[hh * 64:(hh + 1) * 64, :], WCz[:, ri, ft, q4, :], g.snb[:, ri, gp, :],
                                 start=(q4 % 2 == 0 and ri == 0), stop=(q4 % 2 == 1 and ri == 1))
                  for ft in range(8):
                      yps = PS[:, bank, ft * NS:(ft + 1) * NS]
                      STT(g.yT[:, ft, :], g.uT[:, ft, :], v8(l, 2)[:, ft:ft + 1], yps, ALU.mult, ALU.add)

              chk(j, l, 3)
              for g in groups:
                  gelu_inplace(g)

              def cons_glu(g, mt, ps):
                  N = g.N
                  ACT(tmpa[:, 0:N], ps, AF.Tanh, bias=bhalf[:, l * 8 + mt:l * 8 + mt + 1], scale=0.5)
                  TS(tmpa[:, 0:N], tmpa[:, 0:N], 0.5, 0.5, ALU.mult, ALU.add)
                  TT(g.hT[:, 8 + mt, :], tmpa[:, 0:N], g.yT[:, mt, :], ALU.mult)

              w_glu_v = w_glu_d[l].rearrange("(k p) c -> p k c", p=128)
              linear(groups, w_glu_v, 8, 1024, lambda g, kc: g.yT[:, kc, :], cons_glu)

              chk(j, l, 4)
              for g in groups:
                  rmsnorm(g, g.hT[:, 0:8, :], 8, v8(l, 0), g.hT[:, 0:8, :], 1024.0)
                  rmsnorm(g, g.hT[:, 8:16, :], 8, v8(l, 1), g.hT[:, 8:16, :], 1024.0)

              def cons_res(g, mt, ps):
                  TT(g.xT[:, mt, :], ps, g.xT[:, mt, :], ALU.add)

              w_out_v = w_out_d[l].rearrange("(k p) c -> p k c", p=128)
              linear(groups, w_out_v, NCH, D, lambda g, kc: g.hT[:, kc, :], cons_res)

              chk(j, l, 5)
              for g in groups:
                  rmsnorm(g, g.xT, NCH, v16(l, 1), g.hT, float(D))
              w_up_v = w_up_d[l].rearrange("(k p) c -> p k c", p=128)
              w_dn_v = w_down_d[l].rearrange("(k p) c -> p k c", p=128)

              def conv_tile(g, i, which, ps, ext, dst):
                  N = g.N
                  ti = i + which * 44
                  if g is GP:
                      CP(ext[:, 0:2], tails[:, l, ti, :])
                      ACT(ext[:, 2:2 + N], ps, AF.Copy)
                      CP(tails[:, l, ti, :], ext[:, N:N + 2])
                      x0, x1, x2 = ext[:, 0:N], ext[:, 1:N + 1], ext[:, 2:N + 2]
                  else:
                      ACT(ext[:, 0:N], ps, AF.Copy)
                      CP(g.cout[:, ti, 0, :], g.sconv[:, ti, 1, :])
                      CP(g.cout[:, ti, 1, :], ext[:, 0:N])
                      x0, x1, x2 = g.sconv[:, ti, 0, :], g.sconv[:, ti, 1, :], ext[:, 0:N]
                  ACT(dst[:, 0:N], x0, AF.Identity, bias=convp[:, 3, ti:ti + 1], scale=convp[:, 0, ti:ti + 1])
                  STT(dst[:, 0:N], x1, convp[:, 1, ti:ti + 1], dst[:, 0:N], ALU.mult, ALU.add)
                  STT(dst[:, 0:N], x2, convp[:, 2, ti:ti + 1], dst[:, 0:N], ALU.mult, ALU.add)

              for hf in range(2):
                  for i2 in range(11):
                      i0 = hf * 22 + i2 * 2
                      t = wtile(None, NCH, 0, 512, parts=[(w_up_v, i0 * 128, 256), (w_up_v, DFF + i0 * 128, 256)])
                      for m in range(2):
                          i = i0 + m
                          for g in groups:
                              N = g.N
                              psg = PS[:, nextbank(), 0:N]
                              for kc in range(NCH):
                                  MM(psg, t[:, kc, m * 128:(m + 1) * 128], g.hT[:, kc, :], start=(kc == 0), stop=(kc == NCH - 1))
                              psv = PS[:, nextbank(), 0:N]
                              for kc in range(NCH):
                                  MM(psv, t[:, kc, 256 + m * 128:256 + (m + 1) * 128], g.hT[:, kc, :], start=(kc == 0),
                                     stop=(kc == NCH - 1))
                              conv_tile(g, i, 0, psg, extg, cg)
                              conv_tile(g, i, 1, psv, extv, cvv)
                              ACT(tmpa[:, 0:N], cg[:, 0:N], AF.Tanh, scale=0.5)
                              STT(tmpb[:, 0:N], cg[:, 0:N], 0.5, cvv[:, 0:N], ALU.mult, ALU.mult)
                              dsta = actT[:, i - hf * 22, 0:N] if g is GP else g.yT_act[:, i - hf * 22, :]
                              STT(dsta, tmpa[:, 0:N], 1.0, tmpb[:, 0:N], ALU.add, ALU.mult)
                  dn_view = w_dn_v[:, hf * 22:(hf + 1) * 22, :]
                  linear(groups, dn_view, 22, D,
                         lambda g, kc: (actT[:, kc, :] if g is GP else g.yT_act[:, kc, :]), cons_res, cwmax=256)
              if last:
                  DMA(convp_o[l], tails[:, l, :, :])
              if j == 0 and not NOSAMP:
                  DMA(convs_o[l], GS.cout[:])
          chk(j, L, 0)
          for g in groups:
              N = g.N
              pst = PS[:, 4, 0:N]
              for c in range(NCH):
                  sq = sqs[c % 2][:, 0:N]
                  ACT(sq, g.xT[:, c, :], AF.Square)
                  MM(pst, ones_b[:], sq, start=(c == 0), stop=(c == NCH - 1))
              ACT(stdt[:, 0:N], pst, AF.Sqrt, bias=eps_t[:, 0:1], scale=1.0 / D)
              RECIP(rstd[:, 0:N], stdt[:, 0:N])
              for c in range(NCH):
                  ob = tmpa if c % 2 == 0 else tmpb
                  STT(ob[:, 0:N], g.xT[:, c, :], v16(L, 0)[:, c:c + 1], rstd[:, 0:N], ALU.mult, ALU.mult)
                  if g is GP:
                      DMA(yT_v[:, c, t0:t0 + NT], ob[:, 0:N])
                  else:
                      DMA(ysT_v[:, c, :], ob[:, 0:N])

    try:
        main_loop()
    except _Stop:
        pass
    if DBG:
        def dump(name, ap2d, dt):
            shp = [int(x) for x in ap2d.shape]
            d = nc.dram_tensor("dbg_" + name, shp, dt, kind="ExternalOutput").ap()
            DMA(d, ap2d)
        for gname, g in (("p", GP), ("s", GS)):
            dump(gname + "_xT", g.xT[:].rearrange("p c n -> p (c n)"), F32)
            dump(gname + "_hT", g.hT[:].rearrange("p c n -> p (c n)"), BF16)
            dump(gname + "_qT", g.qT[:].rearrange("p c n -> p (c n)"), BF16)
            dump(gname + "_uT", g.uT[:].rearrange("p c n -> p (c n)"), BF16)
            dump(gname + "_krot", g.krot[:], F32)
            dump(gname + "_vTf", g.vTf[:], F32)
        dump("p_kT", GP.kT[:], BF16)
        dump("p_Vt", GP.Vt[:].rearrange("p a b -> p (a b)"), BF16)
        dump("s_kT", GS.kT[:].rearrange("p a b -> p (a b)"), BF16)
        dump("s_Vt", GS.Vt[:].rearrange("p a b c -> p (a b c)"), BF16)
        dump("arena", arena[:], F32)
        dump("WBc", ZA[:].rearrange("p a b -> p (a b)"), BF16)
        dump("WCc", ZB[:].rearrange("p a b -> p (a b)"), BF16)
        dump("Aco", Aco[:].rearrange("p a b c -> p (a b c)"), F32)
        dump("Bco", Bco[:].rearrange("p a b c -> p (a b c)"), F32)
        dump("s_snew", GS.snew[:].rearrange("p a b c -> p (a b c)"), F32)
    S.emit(stack)
    stack.close()
    return nc


_NC_CACHE = {}


def _feat(v, n):
    return np.ascontiguousarray(np.asarray(v).reshape(n, 128).T)


def prep(inp):
    f32 = np.float32
    g = {k: np.asarray(v) for k, v in inp.items()}
    perm_heads = [h for m in range(8) for h in (m, 8 + m)]
    qperm = np.concatenate([np.arange(h * 64, (h + 1) * 64) for h in perm_heads])
    w_in = g["w_in"].astype(f32).copy()
    w_in[:, :, :1024] = g["w_in"][:, :, qperm]
    w_out = g["w_out"].astype(f32).copy()
    w_out[:, :1024, :] = g["w_out"][:, qperm, :]
    aog = g["attn_out_norm_g"][:, qperm]
    sinks_p = g["attn_sinks"][:, perm_heads]

    vec16 = np.zeros((128, (2 * L + 1) * 16), f32)
    for l in range(L):
        vec16[:, (l * 2) * 16:(l * 2 + 1) * 16] = _feat(g["attn_norm_g"][l], 16)
        vec16[:, (l * 2 + 1) * 16:(l * 2 + 2) * 16] = _feat(g["ffn_norm_g"][l], 16)
    vec16[:, (2 * L) * 16:(2 * L + 1) * 16] = _feat(g["final_norm_g"], 16)
    vec8 = np.zeros((128, L * 4 * 8), f32)
    for l in range(L):
        vec8[:, (l * 4 + 0) * 8:(l * 4 + 1) * 8] = _feat(aog[l], 8)
        vec8[:, (l * 4 + 1) * 8:(l * 4 + 2) * 8] = _feat(g["ssm_out_norm_g"][l], 8)
        vec8[:, (l * 4 + 2) * 8:(l * 4 + 3) * 8] = _feat(g["ssm_d"][l], 8)
        vec8[:, (l * 4 + 3) * 8:(l * 4 + 4) * 8] = _feat(g["b_glu"][l], 8)
    convp = np.zeros((L, 128, 4, 88), f32)
    for l in range(L):
        convp[l, :, 0:3, :] = g["conv_w"][l].reshape(3, 88, 128).transpose(2, 0, 1)
        convp[l, :, 3, :] = g["conv_b"][l].reshape(88, 128).T
    sinks = np.ascontiguousarray(np.broadcast_to(sinks_p.reshape(1, L * 16), (128, L * 16))).astype(f32)

    half = 32
    inv = (np.float32(10000.0) ** (-(np.arange(half, dtype=f32) / np.float32(half)))).astype(f32)
    pidx = np.arange(128)
    sgn = np.where((pidx % 64) >= 32, -1.0, 1.0).astype(f32)

    def rope_tabs(pos):
        ang = pos.astype(f32)[:, None] * inv[None, :]
        c = np.cos(ang).astype(f32)
        s_ = np.sin(ang).astype(f32)
        cosT = np.ascontiguousarray(c[:, pidx % 32].T)
        sinT = np.ascontiguousarray((s_[:, pidx % 32] * sgn[None, :]).T)
        return cosT.astype(f32), sinT.astype(f32)

    cosT, sinT = rope_tabs(np.arange(SEQ))
    coss, sins = rope_tabs(np.full((NS,), 16384))
    ident = np.eye(128, dtype=f32)
    partner = np.where((pidx % 64) < 32, pidx + 32, pidx - 32)
    pswap = np.zeros((128, 128), f32)
    pswap[partner, pidx] = 1.0
    qi = np.arange(128)[:, None]
    kj = np.arange(256)[None, :]
    valid = (kj >= qi) & (kj <= qi + 128)
    maskN = np.where(valid, 0.0, -60000.0).astype(f32)
    mask0 = np.where(valid & (kj >= 128), 0.0, -60000.0).astype(f32)

    rmask = np.zeros((128, 2), f32)
    rmask[:, 0] = ((pidx // 32) % 2 == 0)
    rmask[:, 1] = ((pidx // 32) % 2 == 1)
    bexp = np.zeros((L, 2, 128, 1024), f32)
    pwb = np.zeros((L, 3, 128, 1024), f32)
    cexp = np.zeros((L, 2, 128, 1024), f32)
    pwc = np.zeros((L, 3, 128, 32), f32)
    for l in range(L):
        for ri, key in enumerate(("ssm_b_re", "ssm_b_im")):
            Bt = g[key][l].reshape(8, 4, 2, 64, 16)
            out = np.zeros((4, 2, 16, 8, 2, 64), f32)
            for gl in range(2):
                out[:, gl, :, :, gl, :] = Bt[:, :, gl].transpose(1, 3, 0, 2)
            bexp[l, ri] = out.reshape(128, 1024)
        for ri, key in enumerate(("ssm_c_re", "ssm_c_im")):
            C = g[key][l].reshape(32, 2, 16, 64)
            out = np.zeros((2, 64, 32, 2, 16), f32)
            for gl in range(2):
                out[gl, :, :, gl, :] = C[:, gl].transpose(2, 0, 1)
            cexp[l, ri] = out.reshape(128, 1024)
        params = [g["ssm_a_re"][l], g["ssm_a_im"][l],
                  np.broadcast_to(g["ssm_log_dt"][l][:, None], (G, P))]
        for k, A in enumerate(params):
            A = np.asarray(A, f32)
            a4 = A.reshape(8, 4, 2, 64).transpose(1, 0, 2, 3)
            pwb[l, k] = np.broadcast_to(a4[:, None], (4, 32, 8, 2, 64)).reshape(128, 1024)
            pwc[l, k] = A.reshape(32, 2, 64).transpose(1, 2, 0).reshape(128, 32)

    shared = dict(w_in=w_in, w_glu=np.ascontiguousarray(g["w_glu"], f32), w_out=w_out,
                  w_up=np.ascontiguousarray(g["w_up"], f32), w_down=np.ascontiguousarray(g["w_down"], f32),
                  vec16=vec16, vec8=vec8, convp=convp, sinks=sinks, cosT=cosT, sinT=sinT, coss=coss, sins=sins,
                  ident=ident, pswap=pswap, maskN=maskN, mask0=mask0, rmask=rmask, bexp=bexp, pwb=pwb, cexp=cexp, pwc=pwc)
    in_maps = []
    for c in range(NCORES):
        sq = c % 4
        bs = slice(NS * c, NS * (c + 1))
        m = dict(shared)
        m["xT"] = np.ascontiguousarray(g["x_prompt"][sq].T, f32)
        m["xsT"] = np.ascontiguousarray(g["x_sample"][bs, 0, :].T, f32)
        ck = g["cache_k"][:, bs].reshape(L, NS, 128, 128)
        cv = g["cache_v"][:, bs].reshape(L, NS, 128, 128)
        m["ck"] = np.ascontiguousarray(ck, f32)
        m["ckT"] = np.ascontiguousarray(ck.transpose(0, 1, 3, 2), f32)
        m["cv"] = np.ascontiguousarray(cv, f32)
        h0 = np.zeros((L, 128, 2, 32, NS), f32)
        for ri, key in enumerate(("state_ssm_re", "state_ssm_im")):
            st = g[key][:, bs].reshape(L, NS, 32, 2, 64)
            h0[:, :, ri] = st.transpose(0, 3, 4, 2, 1).reshape(L, 128, 32, NS)
        m["h0"] = h0
        sc = g["state_conv"][:, bs].reshape(L, NS, 2, 88, 128)
        m["sconv"] = np.ascontiguousarray(sc.transpose(0, 4, 3, 2, 1), f32)
        in_maps.append(m)

    return in_maps


def kernel(**inp):
    f32 = np.float32
    in_maps = prep(inp)
    if "nc" not in _NC_CACHE:
        _NC_CACHE["nc"] = build_nc()
    nc = _NC_CACHE["nc"]
    res = run_bass_kernel_spmd(nc, in_maps, core_ids=list(range(NCORES)))
    R = res.results

    y_prompt = np.zeros((4, SEQ, D), f32)
    y_sample = np.zeros((32, 1, D), f32)
    k_prompt = np.zeros((L, 4, 128, 2, 64), f32)
    v_prompt = np.zeros((L, 4, 128, 2, 64), f32)
    sre_p = np.zeros((L, 4, G, P), f32)
    sim_p = np.zeros((L, 4, G, P), f32)
    conv_p = np.zeros((L, 4, 2, 2 * DFF), f32)
    k_sample = np.zeros((L, 32, 128, 2, 64), f32)
    v_sample = np.zeros((L, 32, 128, 2, 64), f32)
    sre_s = np.zeros((L, 32, G, P), f32)
    sim_s = np.zeros((L, 32, G, P), f32)
    conv_s = np.zeros((L, 32, 2, 2 * DFF), f32)
    for c in range(NCORES):
        r = R[c]
        bs = slice(NS * c, NS * (c + 1))
        y_sample[bs, 0, :] = np.asarray(r["ysT"]).T
        k_sample[:, bs] = np.asarray(r["ks"]).reshape(L, NS, 128, 2, 64)
        v_sample[:, bs] = np.asarray(r["vs"]).reshape(L, NS, 128, 2, 64)
        ss = np.asarray(r["ssms"]).reshape(L, 2, 64, 2, 32, NS)
        st = ss.transpose(0, 3, 5, 4, 1, 2).reshape(L, 2, NS, G, P)
        sre_s[:, bs] = st[:, 0]
        sim_s[:, bs] = st[:, 1]
        cs = np.asarray(r["convs"])
        conv_s[:, bs] = cs.transpose(0, 4, 3, 2, 1).reshape(L, NS, 2, 2 * DFF)
        if c < 4:
            y_prompt[c] = np.asarray(r["yT"]).T
            k_prompt[:, c] = np.asarray(r["kp"]).transpose(0, 2, 1).reshape(L, 128, 2, 64)
            v_prompt[:, c] = np.asarray(r["vp"]).transpose(0, 2, 1).reshape(L, 128, 2, 64)
            sp = np.asarray(r["ssmp"]).reshape(L, 2, 64, 2, 32)
            sp = sp.transpose(0, 3, 4, 1, 2).reshape(L, 2, G, P)
            sre_p[:, c] = sp[:, 0]
            sim_p[:, c] = sp[:, 1]
            cp = np.asarray(r["convpo"])
            conv_p[:, c] = cp.transpose(0, 3, 2, 1).reshape(L, 2, 2 * DFF)
    return (y_prompt, y_sample, k_prompt, v_prompt, sre_p, sim_p, conv_p,
            k_sample, v_sample, sre_s, sim_s, conv_s)
```
